# Optimizing a Trainium2 kernel written in Bass

```python
import math
import jax, jax.numpy as jnp
from jax import lax
import numpy as np

D_MODEL = 1024
BATCH = 8
SEQ = 4096
DEPTH = 4

GRID_W = 64
Q_BLOCK = 128
ROPE_THETA = 500000.0
AXIAL_THETA = 10000.0
LN_EPS = 1e-5
RMS_EPS = 1e-6

A_HEADS = 4
A_QK_DIM = 64
A_V_DIM = 2 * A_QK_DIM
A_WIDTH = A_HEADS * A_V_DIM
A_ROT = A_QK_DIM // 4

B_Q_HEADS = 8
B_KV_HEADS = 2
B_GROUP = B_Q_HEADS // B_KV_HEADS
B_DIM = 64
B_WIDTH = B_Q_HEADS * B_DIM

EVEN_WIDTH = A_WIDTH + B_WIDTH
EV_SPLITS = [
    A_HEADS * 2 * A_QK_DIM,
    A_HEADS * 2 * A_QK_DIM,
    A_HEADS * A_V_DIM,
    B_Q_HEADS * B_DIM,
    B_KV_HEADS * B_DIM,
    B_KV_HEADS * B_DIM,
    EVEN_WIDTH,
]
EV_IN = sum(EV_SPLITS)

C_HEADS = 16
C_NOPE = 64
C_ROPE = 32
C_V = 64
C_Q_LORA = 256
C_KV_LORA = 128
C_WIDTH = C_HEADS * C_V
OD_SPLITS = [C_Q_LORA, C_KV_LORA, C_ROPE, C_WIDTH]
OD_IN = sum(OD_SPLITS)

N_EVEN = (DEPTH + 1) // 2
N_ODD = DEPTH // 2
ALPHA = (2 * DEPTH) ** 0.25
BETA = (8 * DEPTH) ** -0.25

kernel_name = "hybrid_diffattn_gqa_mla_deepnorm_encoder"


def _split_points(sizes):
    return [int(v) for v in np.cumsum(sizes)[:-1]]


def _rms(x, g):
    xf = x.astype(jnp.float32)
    y = xf * lax.rsqrt(jnp.mean(xf * xf, axis=-1, keepdims=True) + RMS_EPS)
    return (y * g.astype(jnp.float32)).astype(x.dtype)


def _layernorm(x, g, b):
    xf = x.astype(jnp.float32)
    mu = jnp.mean(xf, axis=-1, keepdims=True)
    var = jnp.mean(jnp.square(xf - mu), axis=-1, keepdims=True)
    y = (xf - mu) * lax.rsqrt(var + LN_EPS)
    return (y * g.astype(jnp.float32) + b.astype(jnp.float32)).astype(x.dtype)


def _rope_angles(pos, dims, theta):
    inv = theta ** (-jnp.arange(0, dims, 2, dtype=jnp.float32) / dims)
    ang = pos.astype(jnp.float32)[:, None] * inv[None, :]
    return jnp.cos(ang), jnp.sin(ang)


def _rotate(x, cos, sin):
    xf = x.astype(jnp.float32)
    x1, x2 = jnp.split(xf, 2, axis=-1)
    out = jnp.concatenate([x1 * cos - x2 * sin, x2 * cos + x1 * sin], axis=-1)
    return out.astype(x.dtype)


def _partial_rope(x, cos, sin, rot):
    return jnp.concatenate([_rotate(x[..., :rot], cos, sin), x[..., rot:]], axis=-1)


def _axial_rope(x, row_cs, col_cs):
    half = x.shape[-1] // 2
    return jnp.concatenate([_rotate(x[..., :half], *row_cs),
                            _rotate(x[..., half:], *col_cs)], axis=-1)


def _to_blocks(q):
    *lead, s, d = q.shape
    qb = q.reshape(*lead, s // Q_BLOCK, Q_BLOCK, d)
    return jnp.moveaxis(qb, -3, 0)


def _from_blocks(o):
    o = jnp.moveaxis(o, 0, -3)
    *lead, nb, blk, d = o.shape
    return o.reshape(*lead, nb * blk, d)


def _grouped_attention(q, k, v, scale):
    def one(qb):
        s = jnp.einsum('bgrqd,bgkd->bgrqk', qb, k,
                       preferred_element_type=jnp.float32) * scale
        p = jax.nn.softmax(s, axis=-1)
        return jnp.einsum('bgrqk,bgkd->bgrqd', p.astype(v.dtype), v)
    return _from_blocks(lax.map(one, _to_blocks(q)))


def _diff_attention(q, k, v, lam, scale):
    def one(qb):
        s = jnp.einsum('bhcqd,bhckd->bhcqk', qb, k,
                       preferred_element_type=jnp.float32) * scale
        p = jax.nn.softmax(s, axis=-1)
        pd = p[:, :, 0] - lam * p[:, :, 1]
        return jnp.einsum('bhqk,bhkd->bhqd', pd.astype(v.dtype), v)
    return _from_blocks(lax.map(one, _to_blocks(q)))


def _even_layer(x, w_in, w_out, lam_p, subln_g, qn_g, kn_g, ln_g, ln_b,
                layer_idx, rope_a, row_cs, col_cs):
    bsz, s, _ = x.shape
    h = jnp.einsum('bsd,de->bse', x, w_in)
    qa, ka, va, qb, kb, vb, gate = jnp.split(h, _split_points(EV_SPLITS), axis=-1)

    qa = qa.reshape(bsz, s, A_HEADS, 2, A_QK_DIM).transpose(0, 2, 3, 1, 4)
    ka = ka.reshape(bsz, s, A_HEADS, 2, A_QK_DIM).transpose(0, 2, 3, 1, 4)
    va = va.reshape(bsz, s, A_HEADS, A_V_DIM).transpose(0, 2, 1, 3)
    qa = _partial_rope(qa, *rope_a, A_ROT)
    ka = _partial_rope(ka, *rope_a, A_ROT)
    lam_init = 0.8 - 0.6 * math.exp(-0.3 * layer_idx)
    lp = lam_p.astype(jnp.float32)
    lam = (jnp.exp(jnp.sum(lp[0] * lp[1])) - jnp.exp(jnp.sum(lp[2] * lp[3]))
           + lam_init)
    oa = _diff_attention(qa, ka, va, lam, A_QK_DIM ** -0.5)
    oa = _rms(oa, subln_g) * (1.0 - lam_init)
    oa = oa.transpose(0, 2, 1, 3).reshape(bsz, s, A_WIDTH)

    qb = _rms(qb.reshape(bsz, s, B_Q_HEADS, B_DIM), qn_g).transpose(0, 2, 1, 3)
    kb = _rms(kb.reshape(bsz, s, B_KV_HEADS, B_DIM), kn_g).transpose(0, 2, 1, 3)
    vb = vb.reshape(bsz, s, B_KV_HEADS, B_DIM).transpose(0, 2, 1, 3)
    qb = _axial_rope(qb, row_cs, col_cs).reshape(bsz, B_KV_HEADS, B_GROUP, s, B_DIM)
    kb = _axial_rope(kb, row_cs, col_cs)
    ob = _grouped_attention(qb, kb, vb, B_DIM ** -0.5)
    ob = ob.transpose(0, 3, 1, 2, 4).reshape(bsz, s, B_WIDTH)

    o = jnp.concatenate([oa, ob], axis=-1) * jax.nn.silu(gate)
    y = jnp.einsum('bse,ed->bsd', o, w_out)
    return _layernorm(ALPHA * x + y, ln_g, ln_b)


def _odd_layer(x, w_in, q_norm, kv_norm, w_qb, w_kvb, w_out, ln_g, ln_b, rope_c):
    bsz, s, _ = x.shape
    h = jnp.einsum('bsd,de->bse', x, w_in)
    cq, ckv, kr, gate = jnp.split(h, _split_points(OD_SPLITS), axis=-1)

    cq = _rms(cq, q_norm)
    q = jnp.einsum('bsc,ce->bse', cq, w_qb).reshape(bsz, s, C_HEADS, C_NOPE + C_ROPE)
    q = q.transpose(0, 2, 1, 3)
    q = jnp.concatenate([q[..., :C_NOPE], _rotate(q[..., C_NOPE:], *rope_c)], axis=-1)

    ckv = _rms(ckv, kv_norm)
    kv = jnp.einsum('bsc,ce->bse', ckv, w_kvb).reshape(bsz, s, C_HEADS, C_NOPE + C_V)
    kv = kv.transpose(0, 2, 1, 3)
    k_nope, v = kv[..., :C_NOPE], kv[..., C_NOPE:]
    kr = _rotate(kr[:, None], *rope_c)
    kr = jnp.broadcast_to(kr, (bsz, C_HEADS, s, C_ROPE))
    k = jnp.concatenate([k_nope, kr], axis=-1)

    o = _grouped_attention(q[:, :, None], k, v, (C_NOPE + C_ROPE) ** -0.5)[:, :, 0]
    o = o.transpose(0, 2, 1, 3).reshape(bsz, s, C_WIDTH) * jax.nn.silu(gate)
    y = jnp.einsum('bse,ed->bsd', o, w_out)
    return _layernorm(ALPHA * x + y, ln_g, ln_b)


def setup_inputs(seed: int = 0) -> dict:
    key = jax.random.key(seed)
    ks = jax.random.split(key, 20)

    def nrm(k, shape, scale):
        return jax.random.normal(k, shape, jnp.float32) * scale

    return {
        "x": nrm(ks[0], (BATCH, SEQ, D_MODEL), 1.0),
        "ev_w_in": nrm(ks[1], (N_EVEN, D_MODEL, EV_IN), D_MODEL ** -0.5),
        "ev_w_out": nrm(ks[2], (N_EVEN, EVEN_WIDTH, D_MODEL), BETA * EVEN_WIDTH ** -0.5),
        "ev_lam": nrm(ks[3], (N_EVEN, 4, A_QK_DIM), 0.1),
        "ev_subln": 1.0 + nrm(ks[4], (N_EVEN, A_V_DIM), 0.02),
        "ev_qnorm": 1.0 + nrm(ks[5], (N_EVEN, B_DIM), 0.02),
        "ev_knorm": 1.0 + nrm(ks[6], (N_EVEN, B_DIM), 0.02),
        "ev_ln_g": 1.0 + nrm(ks[7], (N_EVEN, D_MODEL), 0.02),
        "ev_ln_b": nrm(ks[8], (N_EVEN, D_MODEL), 0.02),
        "od_w_in": nrm(ks[9], (N_ODD, D_MODEL, OD_IN), D_MODEL ** -0.5),
        "od_qnorm": 1.0 + nrm(ks[10], (N_ODD, C_Q_LORA), 0.02),
        "od_kvnorm": 1.0 + nrm(ks[11], (N_ODD, C_KV_LORA), 0.02),
        "od_w_qb": nrm(ks[12], (N_ODD, C_Q_LORA, C_HEADS * (C_NOPE + C_ROPE)), C_Q_LORA ** -0.5),
        "od_w_kvb": nrm(ks[13], (N_ODD, C_KV_LORA, C_HEADS * (C_NOPE + C_V)), C_KV_LORA ** -0.5),
        "od_w_out": nrm(ks[14], (N_ODD, C_WIDTH, D_MODEL), BETA * C_WIDTH ** -0.5),
        "od_ln_g": 1.0 + nrm(ks[15], (N_ODD, D_MODEL), 0.02),
        "od_ln_b": nrm(ks[16], (N_ODD, D_MODEL), 0.02),
    }


def reference(x, ev_w_in, ev_w_out, ev_lam, ev_subln, ev_qnorm, ev_knorm,
              ev_ln_g, ev_ln_b, od_w_in, od_qnorm, od_kvnorm, od_w_qb, od_w_kvb,
              od_w_out, od_ln_g, od_ln_b):
    s = x.shape[1]
    rows = s // GRID_W
    pos = jnp.arange(s, dtype=jnp.int32)
    row = jnp.repeat(jnp.arange(rows, dtype=jnp.int32), GRID_W)
    col = jnp.tile(jnp.arange(GRID_W, dtype=jnp.int32), rows)

    rope_a = _rope_angles(pos, A_ROT, ROPE_THETA)
    rope_c = _rope_angles(pos, C_ROPE, ROPE_THETA)
    row_cs = _rope_angles(row, B_DIM // 2, AXIAL_THETA)
    col_cs = _rope_angles(col, B_DIM // 2, AXIAL_THETA)

    for layer in range(DEPTH):
        i = layer // 2
        if layer % 2 == 0:
            x = _even_layer(x, ev_w_in[i], ev_w_out[i], ev_lam[i], ev_subln[i],
                            ev_qnorm[i], ev_knorm[i], ev_ln_g[i], ev_ln_b[i],
                            layer, rope_a, row_cs, col_cs)
        else:
            x = _odd_layer(x, od_w_in[i], od_qnorm[i], od_kvnorm[i], od_w_qb[i],
                           od_w_kvb[i], od_w_out[i], od_ln_g[i], od_ln_b[i], rope_c)
    return x
```

```python
import math
import numpy as np
from contextlib import ExitStack
import concourse.bass as bass
import concourse.mybir as mybir
from concourse.bass_utils import run_bass_kernel_spmd

F32 = mybir.dt.float32
BF16 = mybir.dt.bfloat16
AF = mybir.ActivationFunctionType
ALU = mybir.AluOpType

S = 4096
D = 1024
NTT = 8
NKT = 32
DEPTH = 4
ALPHA = (2 * DEPTH) ** 0.25
LN_EPS = 1e-5
RMS_EPS = 1e-6
WCOLS = 832


class Op:
    __slots__ = ("eng", "fn", "deps", "kind", "key", "cnt", "sig", "need")


class Prog:
    ENGS = ("pe", "act", "dve", "pool", "sp")

    def __init__(self):
        self.ops = {e: [] for e in self.ENGS}
        self.lastw = {}
        self.readers = {}
        self.dcnt = {}
        self.last_dma = {}
        self.pending = {e: set() for e in self.ENGS}

    def _add(self, op, r, w):
        w = list(w) + [k for k in r if k[0] == "ps"]
        r = [k for k in r if k[0] != "ps"]
        deps = set()
        for k in r:
            o = self.lastw.get(k)
            if o is not None:
                deps.add(o)
        for k in w:
            o = self.lastw.get(k)
            if o is not None:
                deps.add(o)
            rd = self.readers.get(k)
            if rd:
                deps.update(rd.values())
        if self.pending[op.eng]:
            deps |= self.pending[op.eng]
            self.pending[op.eng] = set()
        deps.discard(op)
        op.deps = deps
        rk = (op.eng if op.kind == "c" else ("d", op.key))
        for k in r:
            self.readers.setdefault(k, {})[rk] = op
        for k in w:
            self.lastw[k] = op
            self.readers[k] = {}
        self.ops[op.eng].append(op)
        return op

    def op(self, eng, fn, r=(), w=()):
        o = Op()
        o.eng = eng; o.fn = fn; o.kind = "c"; o.key = None; o.cnt = 0; o.sig = 0; o.need = False
        return self._add(o, r, w)

    def dma(self, eng, fn, key, r=(), w=()):
        o = Op()
        o.eng = eng; o.fn = fn; o.kind = "d"; o.key = key; o.sig = 0; o.need = True
        self.dcnt[key] = self.dcnt.get(key, 0) + 16
        o.cnt = self.dcnt[key]
        self.last_dma[key] = o
        return self._add(o, r, w)

    def barrier(self):
        deps = set()
        for e in self.ENGS:
            for o in reversed(self.ops[e]):
                if o.kind == "c":
                    deps.add(o)
                    break
        for o in self.last_dma.values():
            deps.add(o)
        for e in self.ENGS:
            self.pending[e] = set(deps)

    def finalize(self, nc, es):
        for e in self.ENGS:
            for o in self.ops[e]:
                for d in o.deps:
                    if d.kind == "c" and not (d.eng == "pe" and o.eng == "pe" and o.kind == "c"):
                        d.need = True
        self.sem = {}
        for e in self.ENGS:
            n = 0
            for o in self.ops[e]:
                if o.kind == "c" and o.need:
                    n += 1
                    o.sig = n
            self.sem[e] = es.enter_context(nc.semaphore("s_" + e))
        self.dsem = {}
        for i, key in enumerate(self.dcnt):
            self.dsem[key] = es.enter_context(nc.semaphore("d%d" % i))

    def emit(self, engname, e, final_keys=()):
        waited = {}
        for o in self.ops[engname]:
            ws = {}
            for d in o.deps:
                if d.kind == "c":
                    if d.eng == "pe" and engname == "pe" and o.kind == "c":
                        continue
                    sk = ("c", d.eng); val = d.sig
                else:
                    sk = ("d", d.key); val = d.cnt
                if ws.get(sk, 0) < val:
                    ws[sk] = val
            for sk, val in ws.items():
                if waited.get(sk, 0) >= val:
                    continue
                waited[sk] = val
                sem = self.sem[sk[1]] if sk[0] == "c" else self.dsem[sk[1]]
                e.wait_ge(sem, val)
            ins = o.fn(e)
            if o.kind == "c":
                if o.need:
                    ins.then_inc(self.sem[engname], 1)
            else:
                ins.then_inc(self.dsem[o.key], 16)
        for key in final_keys:
            if key in self.dcnt:
                e.wait_ge(self.dsem[key], self.dcnt[key])


def build_program(layers, from_x=True):
    nc = bass.Bass("TRN2", target_bir_lowering=False)
    P = Prog()

    def dram(name, shape, kind="ExternalInput"):
        return nc.dram_tensor(name, shape, F32, kind=kind).ap()

    x_d = dram("x", [S, D])
    out_d = dram("out", [S, D], kind="ExternalOutput")
    wA_d = dram("wA", [2, 4, 128, 8 * 512])
    wBk_d = dram("wBk", [2, 2, 128, 8 * 192])
    wBq_d = dram("wBq", [2, 4, 128, 8 * 256])
    wEo_d = dram("wEo", [2, 128, 8 * 1024])
    wG1_d = dram("wG1", [2, 128, 8 * 768])
    wG2_d = dram("wG2", [2, 128, 8 * 832])
    wQb_d = dram("wQb", [2, 16, 128, 2 * 192])
    wKvb_d = dram("wKvb", [2, 128, 2048])
    wOo_d = dram("wOo", [2, 128, 8 * 1024])
    colsE_d = dram("colsE", [2, 128, 8])
    lamE_d = dram("lamE", [2, 128, 256])
    lnE_d = dram("lnE", [2, 128, 2048])
    colsO_d = dram("colsO", [2, 128, 8])
    lnO_d = dram("lnO", [2, 128, 2048])
    rope_d = dram("rope", [6, 128, S])
    cst_d = dram("cst", [128, 256])
    perm_d = dram("perm", [128, 256])

    es = ExitStack()

    def sb(name, shape, dt):
        return es.enter_context(nc.sbuf_tensor(name, shape, dt))

    XT = sb("XT", [128, 8 * S], BF16)
    OG = sb("OG", [128, 8 * S], BF16)
    QK = sb("QK", [128, max(2 * S, 8192)], BF16)
    AR = sb("AR", [128, 7168], BF16)
    WBF = sb("WBF", [128, 8 * WCOLS], BF16)
    WST = sb("WST", [128, 2 * 832], F32)
    ROPE = sb("ROPE", [128, 1024], F32)
    TMP = sb("TMP", [128, 3072], F32)
    LNP = sb("LNP", [128, 2048], F32)
    CST = sb("CST", [128, 256], F32)
    ONESF = sb("ONESF", [128, 128], F32)
    ONESB = sb("ONESB", [128, 128], BF16)
    COLS = sb("COLS", [128, 8], F32)
    LAM = sb("LAM", [128, 256], F32)
    PERMF = sb("PERMF", [128, 256], F32)
    PERM = sb("PERM", [128, 256], BF16)
    SM = sb("SM", [128, 32], F32)
    PS = es.enter_context(nc.psum_tensor("PS", [128, 4096], F32))

    XT3 = XT[:, :].rearrange("p (c t) -> p c t", c=8)
    qT = QK[:, 0:S]
    kT = QK[:, S:2 * S]
    Vb = AR[:, 0:NKT * 128]
    V3 = Vb.rearrange("p (k d) -> p k d", k=NKT)
    pTb = AR[:, 4096:7168]
    ARF = AR[:, :].bitcast(F32)
    assert tuple(ARF.shape) == (128, 3584), ARF.shape
    ident = CST[:, 0:128]
    blk = CST[:, 128:256]

    def xt(c, a, b):
        return XT[:, c * S + a: c * S + b]

    def og(c, a, b):
        return OG[:, c * S + a: c * S + b]

    def wb(c, a, b):
        return WBF[:, c * WCOLS + a: c * WCOLS + b]

    def bank(b):
        return PS[:, b * 512:(b + 1) * 512]

    def T(i):
        return TMP[:, i * 512:(i + 1) * 512]

    def pT(s):
        return pTb[:, s * 1536:(s + 1) * 1536]

    _bk = [0]

    def nbank():
        b = _bk[0] % 8
        _bk[0] += 1
        return b

    def mm(out, lhsT, rhs, start, stop, r, w, tile_position=None):
        if tile_position is None:
            P.op("pe", lambda e: e.matmul(out, lhsT=lhsT, rhs=rhs, start=start, stop=stop), r=r, w=w)
        else:
            P.op("pe", lambda e: e.matmul(out, lhsT=lhsT, rhs=rhs, start=start, stop=stop,
                                          tile_position=tile_position), r=r, w=w)

    def tr(out, in_, r, w):
        P.op("pe", lambda e: e.transpose(out, in_, ident), r=list(r) + [("c", "cst")], w=w)

    def act(out, in_, func, r, w, scale=1.0, bias=0.0, accum_out=None):
        if accum_out is None:
            P.op("act", lambda e: e.activation(out=out, in_=in_, func=func, bias=bias, scale=scale), r=r, w=w)
        else:
            P.op("act", lambda e: e.activation(out=out, in_=in_, func=func, bias=bias, scale=scale,
                                               accum_out=accum_out), r=r, w=w)

    def tt(eng, out, in0, in1, op, r, w):
        P.op(eng, lambda e: e.tensor_tensor(out=out, in0=in0, in1=in1, op=op), r=r, w=w)

    def stt(out, in0, scalar, in1, op0, op1, r, w):
        P.op("dve", lambda e: e.scalar_tensor_tensor(out=out, in0=in0, scalar=scalar, in1=in1, op0=op0, op1=op1),
             r=r, w=w)

    def ts(eng, out, in0, s1, s2, op0, op1, r, w):
        if s2 is None:
            P.op(eng, lambda e: e.tensor_scalar(out=out, in0=in0, scalar1=s1, scalar2=None, op0=op0), r=r, w=w)
        else:
            P.op(eng, lambda e: e.tensor_scalar(out=out, in0=in0, scalar1=s1, scalar2=s2, op0=op0, op1=op1), r=r, w=w)

    def cp(eng, out, in_, r, w):
        if eng == "act":
            P.op("act", lambda e: e.copy(out=out, in_=in_), r=r, w=w)
        else:
            P.op(eng, lambda e: e.tensor_copy(out=out, in_=in_), r=r, w=w)

    def recip(out, in_, r, w):
        P.op("dve", lambda e: e.reciprocal(out=out, in_=in_), r=r, w=w)

    def memset(eng, ap, val, w):
        P.op(eng, lambda e: e.memset(ap, val), w=w)

    def dma(out, in_, key, r, w, eng="sp"):
        P.dma(eng, lambda e: e.dma_start(out=out, in_=in_), key=key, r=r, w=w)

    KT = lambda t: ("T", t)
    KPS = lambda b: ("ps", b)

    dma(CST[:, :], cst_d[:, :], key="cst", r=[], w=[("c", "cst")])
    memset("pool", ONESF[:, :], 1.0, w=[("c", "onesf")])
    dma(PERMF[:, :], perm_d[:, :], key="perm", r=[], w=[("c", "permf")])
    cp("pool", PERM[:, :], PERMF[:, :], r=[("c", "permf")], w=[("c", "perm")])
    memset("pool", ONESB[:, :], 1.0, w=[("c", "onesb")])

    _wst = [0]

    WK_ALL = ("wbf", "wbf1", "wbf2")

    def load_w(entries, keys=WK_ALL):
        for (src, c, c0) in entries:
            n = src.shape[-1]
            assert n <= 832
            s = _wst[0] % 2
            _wst[0] += 1
            st = WST[:, s * 832: s * 832 + n]
            dma(st, src, key=("wst", s), r=[], w=[("wst", s)])
            cp("pool", wb(c, c0, c0 + n), st, r=[("wst", s)], w=[(k_, c) for k_ in keys])

    def load_rope(tbl, t):
        dma(ROPE[:, 0:512], rope_d[2 * tbl, :, t * 512:(t + 1) * 512], key="ropeC", r=[], w=[("ropeC",)])
        dma(ROPE[:, 512:1024], rope_d[2 * tbl + 1, :, t * 512:(t + 1) * 512], key="ropeS", r=[], w=[("ropeS",)])

    ropeC = ROPE[:, 0:512]
    ropeS = ROPE[:, 512:1024]

    def proj_fm(bk, wcol0, M, t, nch=8, src=None, src_keys=None, wkey="wbf"):
        for c in range(nch):
            rhs = xt(c, t * 512, (t + 1) * 512) if src is None else src(c)
            mm(bank(bk)[0:M, :], wb(c, wcol0, wcol0 + M), rhs, c == 0, c == nch - 1,
               r=[(wkey, c), ("xT", t)], w=[KPS(bk)])

    def silu_to_og(bk, slab, t):
        act(og(slab, t * 512, (t + 1) * 512), bank(bk), AF.Silu, r=[KPS(bk)], w=[("og", slab, t)])

    def rstd_from_sq(src_ps_list, bss, lhs_const, lhs_key, inv_n, eps, out_t, nrows=128):
        for (b, ti) in src_ps_list:
            act(T(ti), bank(b), AF.Square, r=[KPS(b)], w=[KT(ti)])
        n = len(src_ps_list)
        for i, (b, ti) in enumerate(src_ps_list):
            mm(bank(bss), lhs_const, T(ti), i == 0, i == n - 1, r=[("c", lhs_key), KT(ti)], w=[KPS(bss)])
        act(T(out_t), bank(bss), AF.Ln, r=[KPS(bss)], w=[KT(out_t)], scale=inv_n, bias=eps)
        act(T(out_t), T(out_t), AF.Exp, r=[KT(out_t)], w=[KT(out_t)], scale=-0.5)

    def attention(kind, scale, slab, parity=0, lam_col=None, g_col=None, a_const=1.0):
        if kind == "A":
            groups = [[k] for k in range(NKT)]
            SG = [[0, 1], [2, 3]]
        else:
            groups = [list(range(g * 3, min(NKT, g * 3 + 3))) for g in range((NKT + 2) // 3)]
            SG = [[0, 1, 2], [3, 4, 5]]
        flat = [(q, gi, kts) for q in range(NTT) for gi, kts in enumerate(groups)]
        if kind == "A":
            rows = [(0, 64), (64, 128)]
        elif kind == "B":
            rows = [(parity * 64, parity * 64 + 64)]
        else:
            rows = [(0, 96)]

        sc_used = {}

        def scores(idx):
            q, gi, kts = flat[idx]
            sg = idx % 2
            used = []
            if kind == "A":
                k = kts[0]
                for c in range(2):
                    b = SG[sg][c]
                    a0, a1 = rows[c]
                    mm(bank(b), kT[a0:a1, k * 128:(k + 1) * 128], qT[a0:a1, q * 512:(q + 1) * 512], True, True,
                       r=[("kT", k // 4), ("qT", q)], w=[KPS(b)])
                    used.append(b)
            else:
                a0, a1 = rows[0]
                for i, k in enumerate(kts):
                    b = SG[sg][i]
                    rk = [("kT", k // 4), ("qT", q)]
                    if kind == "C":
                        rk.append(("kr", k // 4))
                        rk.append(("qTr", q))
                    mm(bank(b), kT[a0:a1, k * 128:(k + 1) * 128], qT[a0:a1, q * 512:(q + 1) * 512], True, True,
                       r=rk, w=[KPS(b)])
                    used.append(b)
            sc_used[idx] = used

        def scores_exp(idx):
            sg = idx % 2
            used = sc_used.pop(idx)
            n = len(used)
            b0 = used[0]
            act(pT(sg)[:, 0:n * 512], PS[:, b0 * 512:(b0 + n) * 512], AF.Exp,
                r=[KPS(b) for b in used], w=[("pT", sg)], scale=scale)

        def pv(idx):
            q, gi, kts = flat[idx]
            sg = idx % 2
            first = gi == 0
            last = gi == len(groups) - 1
            if kind == "A":
                k = kts[0]
                rr = [("V", k // 4), ("pT", sg)]
                mm(bank(4), V3[:, k, :], pT(sg)[:, 0:512], first, last, r=rr, w=[KPS(4)])
                mm(bank(6), V3[:, k, :], pT(sg)[:, 512:1024], first, last, r=rr, w=[KPS(6)])
                mm(bank(5)[0:64, :], ONESB[:, 0:64], pT(sg)[:, 0:512], first, last,
                   r=[("c", "onesb"), ("pT", sg)], w=[KPS(5)])
                mm(bank(5)[64:128, :], ONESB[:, 0:64], pT(sg)[:, 512:1024], first, last,
                   r=[("c", "onesb"), ("pT", sg)], w=[KPS(5)], tile_position=(0, 64))
            else:
                acc = 6 + (q % 2)
                for i, k in enumerate(kts):
                    mm(bank(acc), V3[:, k, :], pT(sg)[:, i * 512:(i + 1) * 512], first and i == 0,
                       last and i == len(kts) - 1, r=[("V", k // 4), ("Vones",), ("pT", sg)], w=[KPS(acc)])
            if last:
                epilogue(q)

        def epilogue(q):
            qs = (q * 512, (q + 1) * 512)
            if kind == "A":
                cp("dve", T(0), bank(4), r=[KPS(4)], w=[KT(0)])
                cp("dve", T(1), bank(6), r=[KPS(6)], w=[KT(1)])
                cp("dve", T(2), bank(5), r=[KPS(5)], w=[KT(2)])
                recip(T(2), T(2), r=[KT(2)], w=[KT(2)])
                lo_, hi_ = slice(0, 64), slice(64, 128)
                cp("dve", T(3)[lo_, :], T(2)[hi_, :], r=[KT(2)], w=[KT(3)])
                cp("dve", T(3)[hi_, :], T(2)[lo_, :], r=[KT(2)], w=[KT(3)])
                tt("dve", T(0)[lo_, :], T(0)[lo_, :], T(2)[lo_, :], ALU.mult, r=[KT(0), KT(2)], w=[KT(0)])
                tt("dve", T(0)[hi_, :], T(0)[hi_, :], T(3)[hi_, :], ALU.mult, r=[KT(0), KT(3)], w=[KT(0)])
                tt("dve", T(1)[lo_, :], T(1)[lo_, :], T(3)[lo_, :], ALU.mult, r=[KT(1), KT(3)], w=[KT(1)])
                tt("dve", T(1)[hi_, :], T(1)[hi_, :], T(2)[hi_, :], ALU.mult, r=[KT(1), KT(2)], w=[KT(1)])
                stt(T(0), T(1), lam_col, T(0), ALU.mult, ALU.add, r=[KT(0), KT(1), ("c", "lam")], w=[KT(0)])
                deferred.append([min(10, max(1, NKT // 2 - 1)), q, 0])
            else:
                acc = 6 + (q % 2)
                lo = slice(parity * 64, parity * 64 + 64)
                recip(T(2)[64:128, :], bank(acc)[64:128, :], r=[KPS(acc)], w=[KT(2)])
                tt("dve", T(3)[lo, :], bank(acc)[0:64, :], T(2)[64:128, :], ALU.mult, r=[KPS(acc), KT(2)], w=[KT(3)])
                o = og(slab, qs[0], qs[1])[lo, :]
                tt("dve", o, T(3)[lo, :], o, ALU.mult, r=[KT(3), ("og", slab, q)], w=[("og", slab, q)])

        deferred = []

        def epi_stage(q, stage, b):
            qs = (q * 512, (q + 1) * 512)
            if stage == 0:
                act(T(1), T(0), AF.Square, r=[KT(0)], w=[KT(1)])
            elif stage == 1:
                b = 7
                mm(bank(b), ONESF[:, :], T(1), True, True, r=[("c", "onesf"), KT(1)], w=[KPS(b)])
                act(T(1), bank(b), AF.Ln, r=[KPS(b)], w=[KT(1)], scale=1.0 / 128.0, bias=RMS_EPS)
                act(T(1), T(1), AF.Exp, r=[KT(1)], w=[KT(1)], scale=-0.5)
            else:
                stt(T(0), T(0), g_col, T(1), ALU.mult, ALU.mult, r=[KT(0), KT(1), ("c", "cols")], w=[KT(0)])
                o = og(slab, qs[0], qs[1])
                stt(o, T(0), a_const, o, ALU.mult, ALU.mult, r=[KT(0), ("og", slab, q)], w=[("og", slab, q)])

        epi_bank = {}

        def run_deferred(idx, flush=False):
            while deferred:
                d = deferred[0]
                d[0] -= 1
                if d[0] > 0 and not flush:
                    break
                epi_stage(d[1], d[2], SG[(idx + 1) % 2][0])
                d[2] += 1
                if d[2] > 2:
                    deferred.pop(0)
                else:
                    d[0] = 2 if NKT >= 32 else 1
                if not flush:
                    break

        scores(0)
        scores_exp(0)
        if len(flat) > 1:
            scores(1)
            scores_exp(1)
        for idx in range(len(flat)):
            run_deferred(idx)
            if idx + 2 < len(flat):
                scores(idx + 2)
            pv(idx)
            if idx + 2 < len(flat):
                scores_exp(idx + 2)
        def finish():
            while deferred:
                run_deferred(0, flush=True)

        if kind == "A":
            return finish
        finish()
        return None

    def phase_T():
        for i in range(NKT):
            if i % 2 == 0:
                buf = TMP[:, 0:1024]; bkeys = [KT(0), KT(1)]
            else:
                buf = ARF[:, 0:1024]; bkeys = [("arf", 0)]
            dma(buf, x_d[i * 128:(i + 1) * 128, :], key=("xin", i % 2), r=[], w=bkeys)
            b0 = nbank(); b1 = nbank()
            for c in range(8):
                b = b0 if c < 4 else b1
                tr(bank(b)[:, (c % 4) * 128:(c % 4 + 1) * 128], buf[:, c * 128:(c + 1) * 128], r=bkeys, w=[KPS(b)])
            for hb, b in ((0, b0), (1, b1)):
                cp("dve", XT3[:, hb * 4:(hb + 1) * 4, i * 128:(i + 1) * 128],
                   bank(b).rearrange("p (c t) -> p c t", c=4), r=[KPS(b)], w=[("xT", i // 4)])

    WOUT_PREFETCH = (3 * S + 8192) <= 8 * S

    def prefetch_wout(L):
        if not WOUT_PREFETCH:
            return
        wout_d = wEo_d[L // 2] if L % 2 == 0 else wOo_d[L // 2]
        xk = [("xT", t) for t in range(NTT)]
        for c in range(8):
            for h in range(2):
                s = _wst[0] % 2
                _wst[0] += 1
                st = WST[:, s * 832: s * 832 + 512]
                dma(st, wout_d[:, c * 1024 + h * 512: c * 1024 + (h + 1) * 512], key=("wst", s), r=[],
                    w=[("wst", s)])
                o0 = 3 * S + c * 1024 + h * 512
                cp("pool", XT[:, o0:o0 + 512], st, r=[("wst", s)], w=xk)

    def phase_E(L, wout_d, ln_d, last, first):
        src_d = x_d if first else out_d
        qk_keys_lo = [("qT", t) for t in range(NTT)] + [("qTr", t) for t in range(NTT)]
        qk_keys_hi = [("kT", t) for t in range(NTT)] + [("kr", t) for t in range(NTT)]
        xt_keys = [("xT", t) for t in range(NTT)]
        if WOUT_PREFETCH:
            for q4 in range(4):
                eng = "dve" if q4 % 2 == 0 else "act"
                cp(eng, QK[:, q4 * 2048:(q4 + 1) * 2048], XT[:, 3 * S + q4 * 2048: 3 * S + (q4 + 1) * 2048],
                   r=xt_keys, w=(qk_keys_lo if q4 < 2 else qk_keys_hi))
        else:
            for c in range(8):
                for h in range(2):
                    s = _wst[0] % 2
                    _wst[0] += 1
                    st = WST[:, s * 832: s * 832 + 512]
                    dma(st, wout_d[:, c * 1024 + h * 512: c * 1024 + (h + 1) * 512], key=("wst", s), r=[],
                        w=[("wst", s)])
                    cp("pool", QK[:, c * 1024 + h * 512: c * 1024 + (h + 1) * 512], st, r=[("wst", s)],
                       w=(qk_keys_lo if c < 4 else qk_keys_hi))
        dma(LNP[:, :], ln_d[:, :], key="lnp", r=[], w=[("c", "lnp")])
        lng = LNP[:, 0:1024]
        lnb = LNP[:, 1024:2048]
        junk = ROPE[:, 0:1024]
        sets = [
            (TMP[:, 0:1024], TMP[:, 1024:2048], [KT(0), KT(1)], [KT(2), KT(3)]),
            (ARF[:, 0:1024], ARF[:, 1024:2048], [("arf", 0)], [("arf", 1)]),
            (TMP[:, 2048:3072], ARF[:, 2048:3072], [KT(4), KT(5)], [("arf", 2)]),
        ]
        def ctx(i):
            si = i % 3
            xr, xn, kx, kn = sets[si]
            return slice(i * 128, (i + 1) * 128), xr, xn, kx, kn, SM[:, si * 8: si * 8 + 8], ("sm", si), si

        def ybanks(i):
            return 2 * (i % 3), 2 * (i % 3) + 1

        def e_y(i):
            b0, b1 = ybanks(i)
            for h, b in ((0, b0), (1, b1)):
                for c in range(8):
                    mm(bank(b), og(c, i * 128, (i + 1) * 128), QK[:, c * 1024 + h * 512: c * 1024 + (h + 1) * 512],
                       c == 0, c == 7, r=[("og", c, i // 4)] + (qk_keys_lo if c < 4 else qk_keys_hi), w=[KPS(b)])

        def e_ld(i):
            ts_, xr, xn, kx, kn, sm, ksm, si = ctx(i)
            dma(xr, src_d[ts_, :], key=("xin", si), r=[("xhbm", i)], w=kx)

        def e_r(i):
            ts_, xr, xn, kx, kn, sm, ksm, si = ctx(i)
            b0, b1 = ybanks(i)
            for h, b in ((0, b0), (1, b1)):
                stt(xr[:, h * 512:(h + 1) * 512], xr[:, h * 512:(h + 1) * 512], float(ALPHA), bank(b),
                    ALU.mult, ALU.add, r=kx + [KPS(b)], w=kx)

        def e_stats(i):
            ts_, xr, xn, kx, kn, sm, ksm, si = ctx(i)
            act(junk, xr, AF.Identity, r=kx, w=[ksm, ("junk",)], accum_out=sm[:, 0:1])
            act(junk, xr, AF.Square, r=kx, w=[ksm, ("junk",)], accum_out=sm[:, 1:2])

        def e_smalls(i):
            ts_, xr, xn, kx, kn, sm, ksm, si = ctx(i)
            ts("dve", sm[:, 2:3], sm[:, 0:1], -1.0 / D, None, ALU.mult, None, r=[ksm], w=[ksm])
            tt("dve", sm[:, 3:4], sm[:, 2:3], sm[:, 2:3], ALU.mult, r=[ksm], w=[ksm])
            stt(sm[:, 4:5], sm[:, 1:2], 1.0 / D, sm[:, 3:4], ALU.mult, ALU.subtract, r=[ksm], w=[ksm])

        def e_lnexp(i):
            ts_, xr, xn, kx, kn, sm, ksm, si = ctx(i)
            act(sm[:, 5:6], sm[:, 4:5], AF.Ln, r=[ksm], w=[ksm], bias=LN_EPS)
            act(sm[:, 5:6], sm[:, 5:6], AF.Exp, r=[ksm], w=[ksm], scale=-0.5)
            act(sm[:, 6:7], sm[:, 2:3], AF.Identity, r=[ksm], w=[ksm], scale=sm[:, 5:6])

        def e_nmr(i):
            return

        def e_xn(i):
            ts_, xr, xn, kx, kn, sm, ksm, si = ctx(i)
            act(xn, xr, AF.Identity, r=kx + [ksm], w=kn, scale=sm[:, 5:6], bias=sm[:, 6:7])

        def e_gb(i):
            ts_, xr, xn, kx, kn, sm, ksm, si = ctx(i)
            tt("dve", xn, xn, lng, ALU.mult, r=kn + [("c", "lnp")], w=kn)
            tt("dve", xn, xn, lnb, ALU.add, r=kn + [("c", "lnp")], w=kn)
            dma(out_d[ts_, :], xn, key=("xout", si), r=kn, w=[("xhbm", i)], eng="pool")

        def e_tr(i):
            if last:
                return
            ts_, xr, xn, kx, kn, sm, ksm, si = ctx(i)
            for c in range(8):
                b = 6 if c < 4 else 7
                tr(bank(b)[:, (c % 4) * 128:(c % 4 + 1) * 128], xn[:, c * 128:(c + 1) * 128], r=kn, w=[KPS(b)])

        def e_casts(i):
            if last:
                return
            for hb, b in ((0, 6), (1, 7)):
                cp("act", XT3[:, hb * 4:(hb + 1) * 4, i * 128:(i + 1) * 128],
                   bank(b).rearrange("p (c t) -> p c t", c=4), r=[KPS(b)], w=[("xT", i // 4)])

        plan = [(e_y, 0), (e_smalls, 3), (e_lnexp, 3), (e_gb, 4), (e_tr, 4), (e_nmr, 3), (e_xn, 3),
                (e_r, 2), (e_stats, 2), (e_casts, 4), (e_ld, 0)]
        for k in range(NKT + 4):
            for fn, off in plan:
                i = k - off
                if 0 <= i < NKT:
                    fn(i)

    def w_entries(d_ap, ncols, nch=8, c_base=0, col0=0):
        return [(d_ap[:, c * ncols:(c + 1) * ncols], c_base + c, col0) for c in range(nch)]

    def even_first_weights(i):
        load_w(w_entries(wA_d[i, 0], 512))

    def rope_combine(bq, bqs, out_ap, out_keys, rows=slice(0, 128), tp=(0, 1)):
        ta, tb = tp
        tt("dve", T(ta)[rows, :], bank(bq)[rows, :], ropeC[rows, :], ALU.mult, r=[KPS(bq), ("ropeC",)], w=[KT(ta)])
        tt("dve", T(tb)[rows, :], bank(bqs)[rows, :], ropeS[rows, :], ALU.mult, r=[KPS(bqs), ("ropeS",)], w=[KT(tb)])
        tt("pool", out_ap, T(ta)[rows, :], T(tb)[rows, :], ALU.add, r=[KT(ta), KT(tb)], w=out_keys)

    def rms_rope(bq, bqs, gcol, gswcol, out_ap, out_keys):
        bss = nbank()
        rstd_from_sq([(bq, 2)], bss, blk, "cst", 1.0 / 64.0, RMS_EPS, 3)
        stt(T(0), bank(bq), gcol, ropeC, ALU.mult, ALU.mult, r=[KPS(bq), ("ropeC",), ("c", "cols")], w=[KT(0)])
        stt(T(1), bank(bqs), gswcol, ropeS, ALU.mult, ALU.mult, r=[KPS(bqs), ("ropeS",), ("c", "cols")], w=[KT(1)])
        tt("pool", T(0), T(0), T(1), ALU.add, r=[KT(0), KT(1)], w=[KT(0)])
        tt("pool", out_ap, T(0), T(3), ALU.mult, r=[KT(0), KT(3)], w=out_keys)

    def v_proj(t, wcol0, dv, nch=8, src=None):
        bv = nbank()
        for j in range(4):
            tok = (t * 4 + j) * 128
            for c in range(nch):
                lhs = xt(c, tok, tok + 128) if src is None else src(c, tok)
                mm(bank(bv)[:, j * dv:(j + 1) * dv], lhs, wb(c, wcol0, wcol0 + dv), c == 0, c == nch - 1,
                   r=[("wbf", c), ("xT", t)], w=[KPS(bv)])
        cp("act", V3[:, t * 4:(t + 1) * 4, 0:dv], bank(bv)[:, 0:4 * dv].rearrange("p (k d) -> p k d", k=4),
           r=[KPS(bv)], w=[("V", t)])

    def QB(slot, sub=0):
        return pT(slot)[:, sub * 512:(sub + 1) * 512]

    def a_stage1(t, col0, slot):
        b = nbank()
        proj_fm(b, col0, 128, t)
        cp("act", QB(slot, t % 2), bank(b), r=[KPS(b)], w=[("pT", slot)])
        return b

    def a_stage2(t, b, slot, out_ap, out_keys, tp):
        bs = nbank()
        mm(bank(bs), PERM[:, 0:128], QB(slot, t % 2), True, True, r=[("c", "perm"), ("pT", slot)], w=[KPS(bs)])
        rope_combine(b, bs, out_ap, out_keys, tp=tp)

    def b_stage1(t, col0, slot, gcol, wkey="wbf"):
        b = nbank()
        proj_fm(b, col0, 128, t, wkey=wkey)
        act(QB(slot), bank(b), AF.Copy, r=[KPS(b), ("c", "cols")], w=[("pT", slot)], scale=gcol)
        act(T(2 + 2 * slot), bank(b), AF.Square, r=[KPS(b)], w=[KT(2 + 2 * slot)])
        return b

    def b_stage2(t, b, slot, gcol, out_ap, out_keys):
        bs = nbank(); bss = nbank()
        tsq = 2 + 2 * slot
        trs = 3 + 2 * slot
        mm(bank(bs), PERM[:, 128:256], QB(slot), True, True, r=[("c", "perm"), ("pT", slot)], w=[KPS(bs)])
        mm(bank(bss), blk, T(tsq), True, True, r=[("c", "cst"), KT(tsq)], w=[KPS(bss)])
        act(T(trs), bank(bss), AF.Ln, r=[KPS(bss)], w=[KT(trs)], scale=1.0 / 64.0, bias=RMS_EPS)
        act(T(trs), T(trs), AF.Exp, r=[KT(trs)], w=[KT(trs)], scale=-0.5)
        stt(T(0), bank(b), gcol, ropeC, ALU.mult, ALU.mult, r=[KPS(b), ("ropeC",), ("c", "cols")], w=[KT(0)])
        tt("dve", T(1), bank(bs), ropeS, ALU.mult, r=[KPS(bs), ("ropeS",)], w=[KT(1)])
        tt("pool", T(0), T(0), T(1), ALU.add, r=[KT(0), KT(1)], w=[KT(0)])
        tt("pool", out_ap, T(0), T(trs), ALU.mult, r=[KT(0), KT(trs)], w=out_keys)

    def even_layer(L, nxt_loader):
        i = L // 2
        lam_init = 0.8 - 0.6 * math.exp(-0.3 * L)
        dma(COLS[:, :], colsE_d[i], key="cols", r=[], w=[("c", "cols")])
        dma(LAM[:, :], lamE_d[i], key="lam", r=[], w=[("c", "lamraw")])
        tt("dve", LAM[:, 0:64], LAM[:, 0:64], LAM[:, 64:128], ALU.mult, r=[("c", "lamraw")], w=[("c", "lamraw")])
        tt("dve", LAM[:, 128:192], LAM[:, 128:192], LAM[:, 192:256], ALU.mult, r=[("c", "lamraw")], w=[("c", "lamraw")])
        P.op("dve", lambda e: e.reduce_sum(out=SM[:, 16:17], in_=LAM[:, 0:64], axis=mybir.AxisListType.X),
             r=[("c", "lamraw")], w=[("sm", 2)])
        P.op("dve", lambda e: e.reduce_sum(out=SM[:, 17:18], in_=LAM[:, 128:192], axis=mybir.AxisListType.X),
             r=[("c", "lamraw")], w=[("sm", 2)])
        act(SM[:, 16:18], SM[:, 16:18], AF.Exp, r=[("sm", 2)], w=[("sm", 2)])
        tt("dve", SM[:, 18:19], SM[:, 17:18], SM[:, 16:17], ALU.subtract, r=[("sm", 2)], w=[("sm", 2)])
        ts("dve", SM[:, 19:20], SM[:, 18:19], -lam_init, None, ALU.add, None, r=[("sm", 2)], w=[("c", "lam")])
        neglam = SM[:, 19:20]

        pend = None
        for h in range(4):
            for t in range(NTT):
                bg = nbank()
                proj_fm(bg, 384, 128, t)
                silu_to_og(bg, h, t)
            if pend is not None:
                pend()
                pend = None
            prev = None
            for t in range(NTT + 1):
                cur = None
                if t < NTT:
                    cur = (a_stage1(t, 0, 0), a_stage1(t, 128, 1))
                if prev is not None:
                    tp_ = t - 1
                    load_rope(0, tp_)
                    a_stage2(tp_, prev[0], 0, qT[:, tp_ * 512:(tp_ + 1) * 512], [("qT", tp_), ("qTr", tp_)], (0, 1))
                    a_stage2(tp_, prev[1], 1, kT[:, tp_ * 512:(tp_ + 1) * 512], [("kT", tp_), ("kr", tp_)], (4, 5))
                    v_proj(tp_, 256, 128)
                prev = cur
            if h < 3:
                load_w(w_entries(wA_d[i, h + 1], 512))
            else:
                load_w(w_entries(wBq_d[i, 0], 256, col0=192), keys=("wbf", "wbf1", "wbf2"))
                load_w(w_entries(wBq_d[i, 1], 256, col0=448), keys=("wbf", "wbf2"))
                load_w(w_entries(wBk_d[i, 0], 192), keys=("wbf",))
            pend = attention("A", 0.125, h, lam_col=neglam, g_col=COLS[:, 0:1], a_const=float(1.0 - lam_init))
        pend()

        memset("pool", V3[:, :, 64:128], 1.0, w=[("Vones",)] + [("V", t) for t in range(NTT)])
        for g in range(2):
            prev = None
            for t in range(NTT + 1):
                cur = None
                if t < NTT:
                    cur = b_stage1(t, 0, t % 2, COLS[:, 3:4])
                if prev is not None:
                    tp_ = t - 1
                    load_rope(1, tp_)
                    b_stage2(tp_, prev, tp_ % 2, COLS[:, 3:4], kT[:, tp_ * 512:(tp_ + 1) * 512],
                             [("kT", tp_), ("kr", tp_)])
                    v_proj(tp_, 128, 64)
                prev = cur
            for pp in range(2):
                slab = 4 + g * 2 + pp
                wk_ = "wbf1" if pp == 0 else "wbf2"
                cb_ = 192 + pp * 256
                for t in range(NTT):
                    bg = nbank()
                    proj_fm(bg, cb_ + 128, 128, t, wkey=wk_)
                    silu_to_og(bg, slab, t)
                prev = None
                for t in range(NTT + 1):
                    cur = None
                    if t < NTT:
                        cur = b_stage1(t, cb_, t % 2, COLS[:, 1:2], wkey=wk_)
                    if prev is not None:
                        tp_ = t - 1
                        load_rope(1, tp_)
                        b_stage2(tp_, prev, tp_ % 2, COLS[:, 1:2], qT[:, tp_ * 512:(tp_ + 1) * 512],
                                 [("qT", tp_), ("qTr", tp_)])
                    prev = cur
                if pp == 0:
                    pass
                elif g == 0:
                    load_w(w_entries(wBq_d[i, 2], 256, col0=192), keys=("wbf1",))
                    load_w(w_entries(wBq_d[i, 3], 256, col0=448), keys=("wbf2",))
                    load_w(w_entries(wBk_d[i, 1], 192), keys=("wbf",))
                else:
                    nxt_loader()
                    prefetch_wout(L)
                for jj in range(2):
                    attention("B", 0.125, slab, parity=jj)

    def odd_first_weights(i):
        load_w([(wG1_d[i][:, c * 768: c * 768 + 384], c, 0) for c in range(8)])
        load_w([(wG1_d[i][:, c * 768 + 384: c * 768 + 768], c, 448) for c in range(8)])

    def ropeCp(t):
        return LNP[(t % 4) * 32:(t % 4) * 32 + 32, (t // 4) * 512:(t // 4 + 1) * 512]

    def ropeSp(t):
        return LNP[(t % 4) * 32:(t % 4) * 32 + 32, 1024 + (t // 4) * 512:1024 + (t // 4 + 1) * 512]

    def load_ropep():
        for t in range(NTT):
            dma(ropeCp(t), rope_d[4, 64:96, t * 512:(t + 1) * 512], key="ropep", r=[], w=[("c", "ropep")])
            dma(ropeSp(t), rope_d[5, 64:96, t * 512:(t + 1) * 512], key="ropep", r=[], w=[("c", "ropep")])

    def odd_layer(L, nxt_loader):
        i = L // 2
        load_ropep()
        dma(COLS[:, :], colsO_d[i], key="cols", r=[], w=[("c", "cols")])
        for t in range(NTT):
            for s_ in range(3):
                bg = nbank()
                proj_fm(bg, s_ * 128, 128, t, wkey="wbf")
                silu_to_og(bg, s_, t)
        load_w([(wG2_d[i][:, c * 832: c * 832 + 256], c, 0) for c in range(8)], keys=("wbf",))
        load_w([(wG2_d[i][:, c * 832 + 640: c * 832 + 832], c, 256) for c in range(8)], keys=("wbf",))
        memset("pool", V3[:, :, 64:128], 1.0, w=[("Vones",)] + [("V", t) for t in range(NTT)])
        for t in range(NTT):
            for s_ in range(3):
                bg = nbank()
                proj_fm(bg, 448 + s_ * 128, 128, t, wkey="wbf2")
                silu_to_og(bg, 3 + s_, t)
        load_w([(wG2_d[i][:, c * 832 + 256: c * 832 + 640], c, 448) for c in range(8)], keys=("wbf2",))
        for t in range(NTT):
            for s_ in range(2):
                bg = nbank()
                proj_fm(bg, s_ * 128, 128, t, wkey="wbf")
                silu_to_og(bg, 6 + s_, t)
        for t in range(NTT):
            bkr = nbank(); bkrs = nbank()
            proj_fm(bkr, 256, 96, t, wkey="wbf"); proj_fm(bkrs, 352, 96, t, wkey="wbf")
            r64 = slice(64, 96)
            tt("dve", T(0)[r64, :], bank(bkr)[r64, :], ropeCp(t), ALU.mult, r=[KPS(bkr), ("c", "ropep")], w=[KT(0)])
            tt("dve", T(1)[r64, :], bank(bkrs)[r64, :], ropeSp(t), ALU.mult, r=[KPS(bkrs), ("c", "ropep")], w=[KT(1)])
            tt("pool", kT[64:96, t * 512:(t + 1) * 512], T(0)[r64, :], T(1)[r64, :], ALU.add, r=[KT(0), KT(1)],
               w=[("kr", t)])
        for t in range(NTT):
            bc0 = nbank(); bc1 = nbank(); bkv = nbank()
            proj_fm(bc0, 448, 128, t, wkey="wbf2"); proj_fm(bc1, 576, 128, t, wkey="wbf2")
            proj_fm(bkv, 704, 128, t, wkey="wbf2")
            bss = nbank()
            rstd_from_sq([(bc0, 2), (bc1, 3)], bss, ONESF[:, :], "onesf", 1.0 / 256.0, RMS_EPS, 5)
            stt(xt(0, t * 512, (t + 1) * 512), bank(bc0), COLS[:, 0:1], T(5), ALU.mult, ALU.mult,
                r=[KPS(bc0), KT(5), ("c", "cols")], w=[("xT", t)])
            stt(xt(1, t * 512, (t + 1) * 512), bank(bc1), COLS[:, 1:2], T(5), ALU.mult, ALU.mult,
                r=[KPS(bc1), KT(5), ("c", "cols")], w=[("xT", t)])
            bss2 = nbank()
            rstd_from_sq([(bkv, 2)], bss2, ONESF[:, :], "onesf", 1.0 / 128.0, RMS_EPS, 3)
            stt(xt(2, t * 512, (t + 1) * 512), bank(bkv), COLS[:, 2:3], T(3), ALU.mult, ALU.mult,
                r=[KPS(bkv), KT(3), ("c", "cols")], w=[("xT", t)])

        def c_weights(j, keys=WK_ALL):
            ent = [(wQb_d[i, j][:, c * 192:(c + 1) * 192], c, 0) for c in range(2)]
            ent.append((wKvb_d[i][:, j * 128:(j + 1) * 128], 2, 0))
            load_w(ent, keys=keys)

        c_weights(0, keys=("wbf",))
        sc = 96.0 ** -0.5
        for j in range(16):
            for t in range(NTT):
                bq = nbank(); bqs = nbank()
                proj_fm(bq, 0, 96, t, nch=2); proj_fm(bqs, 96, 96, t, nch=2)
                r64 = slice(64, 96)
                tsl = slice(t * 512, (t + 1) * 512)
                ta, tb = (0, 1) if t % 2 == 0 else (4, 5)
                tt("dve", T(ta)[r64, :], bank(bq)[r64, :], ropeCp(t), ALU.mult, r=[KPS(bq), ("c", "ropep")], w=[KT(ta)])
                tt("dve", T(tb)[r64, :], bank(bqs)[r64, :], ropeSp(t), ALU.mult, r=[KPS(bqs), ("c", "ropep")], w=[KT(tb)])
                cp("act", qT[0:64, tsl], bank(bq)[0:64, :], r=[KPS(bq)], w=[("qT", t)])
                tt("pool", qT[64:96, tsl], T(ta)[r64, :], T(tb)[r64, :], ALU.add, r=[KT(ta), KT(tb)], w=[("qTr", t)])
                bk = nbank()
                mm(bank(bk)[0:64, :], wb(2, 0, 64), xt(2, t * 512, (t + 1) * 512), True, True,
                   r=[("wbf", 2), ("xT", t)], w=[KPS(bk)])
                cp("act", kT[0:64, t * 512:(t + 1) * 512], bank(bk)[0:64, :], r=[KPS(bk)], w=[("kT", t)])
                bv = nbank()
                for jj in range(4):
                    tok = (t * 4 + jj) * 128
                    mm(bank(bv)[:, jj * 64:(jj + 1) * 64], xt(2, tok, tok + 128), wb(2, 64, 128), True, True,
                       r=[("wbf", 2), ("xT", t)], w=[KPS(bv)])
                cp("act", V3[:, t * 4:(t + 1) * 4, 0:64], bank(bv)[:, 0:256].rearrange("p (k d) -> p k d", k=4),
                   r=[KPS(bv)], w=[("V", t)])
            if j < 15:
                c_weights(j + 1)
            else:
                nxt_loader()
                prefetch_wout(L)
            attention("C", sc, j // 2, parity=j % 2)

    def first_weights(L):
        if L % 2 == 0:
            even_first_weights(L // 2)
        else:
            odd_first_weights(L // 2)

    if from_x and layers[0] == 0:
        phase_T()
    else:
        phase_T()
    first_weights(layers[0])
    P.barrier()
    for li, L in enumerate(layers):
        last = li == len(layers) - 1
        nxt = (lambda L2=layers[li + 1]: first_weights(L2)) if not last else (lambda: None)
        if L % 2 == 0:
            even_layer(L, nxt)
        else:
            odd_layer(L, nxt)
        P.barrier()
        if L % 2 == 0:
            phase_E(L, wEo_d[L // 2], lnE_d[L // 2], last, li == 0)
        else:
            phase_E(L, wOo_d[L // 2], lnO_d[L // 2], last, li == 0)
        P.barrier()

    P.finalize(nc, es)
    with nc.Block() as block:
        @block.tensor
        def _(e):
            P.emit("pe", e)

        @block.scalar
        def _(e):
            P.emit("act", e)

        @block.vector
        def _(e):
            P.emit("dve", e)

        @block.gpsimd
        def _(e):
            P.emit("pool", e)

        @block.sync
        def _(e):
            P.emit("sp", e, final_keys=[("xout", 0), ("xout", 1), ("xout", 2)])
    es.close()
    return nc, P


def _tile_w(w, ncols):
    k = w.shape[0]
    return np.ascontiguousarray(w.reshape(k // 128, 128, ncols).transpose(1, 0, 2).reshape(128, -1))


def _gather_cols(w, idx):
    idx = np.asarray(idx)
    o = np.zeros((w.shape[0], len(idx)), np.float32)
    m = idx >= 0
    o[:, m] = w[:, idx[m]]
    return o


def _partner(d, half):
    return d + half if (d % (2 * half)) < half else d - half


def _rope_tables():
    pos = np.arange(S, dtype=np.float32)
    row = (np.arange(S) // 64).astype(np.float32)
    col = (np.arange(S) % 64).astype(np.float32)

    def angles(p, dims, theta):
        inv = np.float32(theta) ** (-(np.arange(0, dims, 2, dtype=np.float32) / np.float32(dims)))
        inv = inv.astype(np.float32)
        ang = (p[None, :] * inv[:, None]).astype(np.float32)
        return np.cos(ang).astype(np.float32), np.sin(ang).astype(np.float32)

    tabs = np.zeros((6, 128, S), np.float32)
    tabs[0::2] = 1.0
    ca, sa = angles(pos, 16, 500000.0)
    for p in range(128):
        d = p % 64
        if d < 16:
            f = d % 8
            tabs[0, p] = ca[f]
            tabs[1, p] = -sa[f] if d < 8 else sa[f]
    cr, sr = angles(row, 32, 10000.0)
    cc, sc = angles(col, 32, 10000.0)
    for p in range(128):
        d = p % 64
        if d < 32:
            f = d % 16
            tabs[2, p] = cr[f]
            tabs[3, p] = -sr[f] if d < 16 else sr[f]
        else:
            dd = d - 32
            f = dd % 16
            tabs[2, p] = cc[f]
            tabs[3, p] = -sc[f] if dd < 16 else sc[f]
    c3, s3 = angles(pos, 32, 500000.0)
    for p in range(64, 96):
        dd = p - 64
        f = dd % 16
        tabs[4, p] = c3[f]
        tabs[5, p] = -s3[f] if dd < 16 else s3[f]
    return tabs


def _prep_weights(inp):
    f = lambda a: np.asarray(a, dtype=np.float32)
    ev_w_in, ev_w_out = f(inp["ev_w_in"]), f(inp["ev_w_out"])
    od_w_in, od_w_out = f(inp["od_w_in"]), f(inp["od_w_out"])
    od_w_qb, od_w_kvb = f(inp["od_w_qb"]), f(inp["od_w_kvb"])
    wA = np.zeros((2, 4, 128, 8 * 512), np.float32)
    wBk = np.zeros((2, 2, 128, 8 * 192), np.float32)
    wBq = np.zeros((2, 4, 128, 8 * 256), np.float32)
    wEo = np.zeros((2, 128, 8 * 1024), np.float32)
    colsE = np.zeros((2, 128, 8), np.float32)
    lamE = np.zeros((2, 128, 256), np.float32)
    lnE = np.zeros((2, 128, 2048), np.float32)
    pa = np.array([(_partner(d, 8) if d < 16 else -1) for d in range(64)])
    pb = np.array([_partner(d, 16) for d in range(64)])
    for i in range(2):
        W = ev_w_in[i]
        for h in range(4):
            q = np.arange(128) + h * 128
            qsw = np.array([(h * 128 + (p // 64) * 64 + pa[p % 64]) if pa[p % 64] >= 0 else -1 for p in range(128)])
            k = q + 512
            ksw = np.where(qsw >= 0, qsw + 512, -1)
            v = np.arange(128) + 1024 + h * 128
            g = np.arange(128) + 2304 + h * 128
            idx = np.concatenate([q, k, v, g])
            wA[i, h] = _tile_w(_gather_cols(W, idx), 512)
        for g_ in range(2):
            k = 2048 + g_ * 64 + np.arange(64)
            ksw = 2048 + g_ * 64 + pb
            v = 2176 + g_ * 64 + np.arange(64)
            idx = np.concatenate([k, k, v])
            wBk[i, g_] = _tile_w(_gather_cols(W, idx), 192)
            for pp in range(2):
                j0 = g_ * 4 + pp * 2
                q = 1536 + j0 * 64 + np.arange(128)
                qsw = np.array([1536 + (j0 + p // 64) * 64 + pb[p % 64] for p in range(128)])
                gt = 2304 + 512 + j0 * 64 + np.arange(128)
                idx = np.concatenate([q, gt])
                wBq[i, g_ * 2 + pp] = _tile_w(_gather_cols(W, idx), 256)
        wEo[i] = _tile_w(ev_w_out[i], 1024)
        qn, kn = f(inp["ev_qnorm"])[i], f(inp["ev_knorm"])[i]
        colsE[i, :, 0] = f(inp["ev_subln"])[i]
        colsE[i, :, 1] = np.tile(qn, 2)
        colsE[i, :, 2] = np.tile(qn[pb], 2)
        colsE[i, :, 3] = np.tile(kn, 2)
        colsE[i, :, 4] = np.tile(kn[pb], 2)
        lamE[i] = np.broadcast_to(f(inp["ev_lam"])[i].reshape(1, 256), (128, 256))
        lnE[i, :, 0:1024] = np.broadcast_to(f(inp["ev_ln_g"])[i][None, :], (128, 1024))
        lnE[i, :, 1024:2048] = np.broadcast_to(f(inp["ev_ln_b"])[i][None, :], (128, 1024))
    wG1 = np.zeros((2, 128, 8 * 768), np.float32)
    wG2 = np.zeros((2, 128, 8 * 832), np.float32)
    wQb = np.zeros((2, 16, 128, 2 * 192), np.float32)
    wKvb = np.zeros((2, 128, 2048), np.float32)
    wOo = np.zeros((2, 128, 8 * 1024), np.float32)
    colsO = np.zeros((2, 128, 8), np.float32)
    lnO = np.zeros((2, 128, 2048), np.float32)
    pc = np.array([_partner(d, 16) for d in range(32)])
    for i in range(2):
        W = od_w_in[i]
        wG1[i] = _tile_w(_gather_cols(W, 416 + np.arange(768)), 768)
        kr = np.concatenate([-np.ones(64, int), 384 + np.arange(32)])
        krsw = np.concatenate([-np.ones(64, int), 384 + pc])
        idx = np.concatenate([416 + 768 + np.arange(256), np.arange(256), 256 + np.arange(128), kr, krsw])
        wG2[i] = _tile_w(_gather_cols(W, idx), 832)
        for j in range(16):
            q = j * 96 + np.arange(96)
            qsw = np.concatenate([-np.ones(64, int), j * 96 + 64 + pc])
            wQb[i, j] = _tile_w(_gather_cols(od_w_qb[i], np.concatenate([q, qsw])), 192)
        wKvb[i] = od_w_kvb[i]
        wOo[i] = _tile_w(od_w_out[i], 1024)
        qn = f(inp["od_qnorm"])[i]
        colsO[i, :, 0] = qn[0:128]
        colsO[i, :, 1] = qn[128:256]
        colsO[i, :, 2] = f(inp["od_kvnorm"])[i]
        lnO[i, :, 0:1024] = np.broadcast_to(f(inp["od_ln_g"])[i][None, :], (128, 1024))
        lnO[i, :, 1024:2048] = np.broadcast_to(f(inp["od_ln_b"])[i][None, :], (128, 1024))
    perm = np.zeros((128, 256), np.float32)
    for p in range(128):
        d = p % 64
        if pa[d] >= 0:
            perm[(p // 64) * 64 + pa[d], p] = 1.0
        perm[(p // 64) * 64 + pb[d], 128 + p] = 1.0
    cst = np.zeros((128, 256), np.float32)
    cst[:, 0:128] = np.eye(128, dtype=np.float32)
    cst[0:64, 128:192] = 1.0
    cst[64:128, 192:256] = 1.0
    return dict(wA=wA, wBk=wBk, wBq=wBq, wEo=wEo, wG1=wG1, wG2=wG2, wQb=wQb, wKvb=wKvb, wOo=wOo,
                colsE=colsE, lamE=lamE, lnE=lnE, colsO=colsO, lnO=lnO, rope=_rope_tables(), cst=cst, perm=perm)


_CACHE = {}


def run_layers(x_full, weights, layers, cores=None, trace=False):
    key = tuple(layers)
    if key not in _CACHE:
        _CACHE[key] = build_program(list(layers))
    nc, _ = _CACHE[key]
    n = x_full.shape[0]
    in_maps = []
    for b in range(n):
        m = dict(weights)
        m["x"] = np.ascontiguousarray(x_full[b], dtype=np.float32)
        in_maps.append(m)
    res = run_bass_kernel_spmd(nc, in_maps, core_ids=list(range(n)), **({"trace": True} if trace else {}))
    return np.stack([r["out"] for r in res.results], axis=0), res


def kernel(**inputs):
    x = np.asarray(inputs["x"], dtype=np.float32)
    weights = _prep_weights(inputs)
    out, _ = run_layers(x, weights, (0, 1, 2, 3))
    return out.astype(np.float32)
```

```python
import math
import numpy as np
from contextlib import ExitStack
import concourse.bass as bass
import concourse.mybir as mybir
from concourse.bass_utils import run_bass_kernel_spmd

F32 = mybir.dt.float32
BF16 = mybir.dt.bfloat16
AF = mybir.ActivationFunctionType
ALU = mybir.AluOpType

S = 4096
D = 1024
NTT = 8
NKT = 32
DEPTH = 4
ALPHA = (2 * DEPTH) ** 0.25
LN_EPS = 1e-5
RMS_EPS = 1e-6
WCOLS = 832


class Op:
    __slots__ = ("eng", "fn", "deps", "kind", "key", "cnt", "sig", "need")


class Prog:
    ENGS = ("pe", "act", "dve", "pool", "sp")

    def __init__(self):
        self.ops = {e: [] for e in self.ENGS}
        self.lastw = {}
        self.readers = {}
        self.dcnt = {}
        self.last_dma = {}
        self.pending = {e: set() for e in self.ENGS}

    def _add(self, op, r, w):
        w = list(w) + [k for k in r if k[0] == "ps"]
        r = [k for k in r if k[0] != "ps"]
        deps = set()
        for k in r:
            o = self.lastw.get(k)
            if o is not None:
                deps.add(o)
        for k in w:
            o = self.lastw.get(k)
            if o is not None:
                deps.add(o)
            rd = self.readers.get(k)
            if rd:
                deps.update(rd.values())
        if self.pending[op.eng]:
            deps |= self.pending[op.eng]
            self.pending[op.eng] = set()
        deps.discard(op)
        op.deps = deps
        rk = (op.eng if op.kind == "c" else ("d", op.key))
        for k in r:
            self.readers.setdefault(k, {})[rk] = op
        for k in w:
            self.lastw[k] = op
            self.readers[k] = {}
        self.ops[op.eng].append(op)
        return op

    def op(self, eng, fn, r=(), w=()):
        o = Op()
        o.eng = eng; o.fn = fn; o.kind = "c"; o.key = None; o.cnt = 0; o.sig = 0; o.need = False
        return self._add(o, r, w)

    def dma(self, eng, fn, key, r=(), w=()):
        o = Op()
        o.eng = eng; o.fn = fn; o.kind = "d"; o.key = key; o.sig = 0; o.need = True
        self.dcnt[key] = self.dcnt.get(key, 0) + 16
        o.cnt = self.dcnt[key]
        self.last_dma[key] = o
        return self._add(o, r, w)

    def barrier(self):
        deps = set()
        for e in self.ENGS:
            for o in reversed(self.ops[e]):
                if o.kind == "c":
                    deps.add(o)
                    break
        for o in self.last_dma.values():
            deps.add(o)
        for e in self.ENGS:
            self.pending[e] = set(deps)

    def finalize(self, nc, es):
        for e in self.ENGS:
            for o in self.ops[e]:
                for d in o.deps:
                    if d.kind == "c" and not (d.eng == "pe" and o.eng == "pe" and o.kind == "c"):
                        d.need = True
        self.sem = {}
        for e in self.ENGS:
            n = 0
            for o in self.ops[e]:
                if o.kind == "c" and o.need:
                    n += 1
                    o.sig = n
            self.sem[e] = es.enter_context(nc.semaphore("s_" + e))
        self.dsem = {}
        for i, key in enumerate(self.dcnt):
            self.dsem[key] = es.enter_context(nc.semaphore("d%d" % i))

    def emit(self, engname, e, final_keys=()):
        waited = {}
        for o in self.ops[engname]:
            ws = {}
            for d in o.deps:
                if d.kind == "c":
                    if d.eng == "pe" and engname == "pe" and o.kind == "c":
                        continue
                    sk = ("c", d.eng); val = d.sig
                else:
                    sk = ("d", d.key); val = d.cnt
                if ws.get(sk, 0) < val:
                    ws[sk] = val
            for sk, val in ws.items():
                if waited.get(sk, 0) >= val:
                    continue
                waited[sk] = val
                sem = self.sem[sk[1]] if sk[0] == "c" else self.dsem[sk[1]]
                e.wait_ge(sem, val)
            ins = o.fn(e)
            if o.kind == "c":
                if o.need:
                    ins.then_inc(self.sem[engname], 1)
            else:
                ins.then_inc(self.dsem[o.key], 16)
        for key in final_keys:
            if key in self.dcnt:
                e.wait_ge(self.dsem[key], self.dcnt[key])


def build_program(layers, from_x=True):
    nc = bass.Bass("TRN2", target_bir_lowering=False)
    P = Prog()

    def dram(name, shape, kind="ExternalInput"):
        return nc.dram_tensor(name, shape, F32, kind=kind).ap()

    x_d = dram("x", [S, D])
    out_d = dram("out", [S, D], kind="ExternalOutput")
    wA_d = dram("wA", [2, 4, 128, 8 * 512])
    wBk_d = dram("wBk", [2, 2, 128, 8 * 192])
    wBq_d = dram("wBq", [2, 4, 128, 8 * 256])
    wEo_d = dram("wEo", [2, 128, 8 * 1024])
    wG1_d = dram("wG1", [2, 128, 8 * 768])
    wG2_d = dram("wG2", [2, 128, 8 * 832])
    wQb_d = dram("wQb", [2, 16, 128, 2 * 192])
    wKvb_d = dram("wKvb", [2, 128, 2048])
    wOo_d = dram("wOo", [2, 128, 8 * 1024])
    colsE_d = dram("colsE", [2, 128, 8])
    lamE_d = dram("lamE", [2, 128, 256])
    lnE_d = dram("lnE", [2, 128, 2048])
    colsO_d = dram("colsO", [2, 128, 8])
    lnO_d = dram("lnO", [2, 128, 2048])
    rope_d = dram("rope", [6, 128, S])
    cst_d = dram("cst", [128, 256])
    perm_d = dram("perm", [128, 256])

    es = ExitStack()

    def sb(name, shape, dt):
        return es.enter_context(nc.sbuf_tensor(name, shape, dt))

    XT = sb("XT", [128, 8 * S], BF16)
    OG = sb("OG", [128, 8 * S], BF16)
    QK = sb("QK", [128, max(2 * S, 8192)], BF16)
    AR = sb("AR", [128, 7168], BF16)
    WBF = sb("WBF", [128, 8 * WCOLS], BF16)
    WST = sb("WST", [128, 2 * 832], F32)
    ROPE = sb("ROPE", [128, 1024], F32)
    TMP = sb("TMP", [128, 3072], F32)
    LNP = sb("LNP", [128, 2048], F32)
    CST = sb("CST", [128, 256], F32)
    ONESF = sb("ONESF", [128, 128], F32)
    ONESB = sb("ONESB", [128, 128], BF16)
    COLS = sb("COLS", [128, 8], F32)
    LAM = sb("LAM", [128, 256], F32)
    PERMF = sb("PERMF", [128, 256], F32)
    PERM = sb("PERM", [128, 256], BF16)
    SM = sb("SM", [128, 32], F32)
    PS = es.enter_context(nc.psum_tensor("PS", [128, 4096], F32))

    XT3 = XT[:, :].rearrange("p (c t) -> p c t", c=8)
    qT = QK[:, 0:S]
    kT = QK[:, S:2 * S]
    Vb = AR[:, 0:NKT * 128]
    V3 = Vb.rearrange("p (k d) -> p k d", k=NKT)
    pTb = AR[:, 4096:7168]
    ARF = AR[:, :].bitcast(F32)
    assert tuple(ARF.shape) == (128, 3584), ARF.shape
    ident = CST[:, 0:128]
    blk = CST[:, 128:256]

    def xt(c, a, b):
        return XT[:, c * S + a: c * S + b]

    def og(c, a, b):
        return OG[:, c * S + a: c * S + b]

    def wb(c, a, b):
        return WBF[:, c * WCOLS + a: c * WCOLS + b]

    def bank(b):
        return PS[:, b * 512:(b + 1) * 512]

    def T(i):
        return TMP[:, i * 512:(i + 1) * 512]

    def pT(s):
        return pTb[:, s * 1536:(s + 1) * 1536]

    _bk = [0]

    def nbank():
        b = _bk[0] % 8
        _bk[0] += 1
        return b

    def mm(out, lhsT, rhs, start, stop, r, w, tile_position=None):
        if tile_position is None:
            P.op("pe", lambda e: e.matmul(out, lhsT=lhsT, rhs=rhs, start=start, stop=stop), r=r, w=w)
        else:
            P.op("pe", lambda e: e.matmul(out, lhsT=lhsT, rhs=rhs, start=start, stop=stop,
                                          tile_position=tile_position), r=r, w=w)

    def tr(out, in_, r, w):
        P.op("pe", lambda e: e.transpose(out, in_, ident), r=list(r) + [("c", "cst")], w=w)

    def act(out, in_, func, r, w, scale=1.0, bias=0.0, accum_out=None):
        if accum_out is None:
            P.op("act", lambda e: e.activation(out=out, in_=in_, func=func, bias=bias, scale=scale), r=r, w=w)
        else:
            P.op("act", lambda e: e.activation(out=out, in_=in_, func=func, bias=bias, scale=scale,
                                               accum_out=accum_out), r=r, w=w)

    def tt(eng, out, in0, in1, op, r, w):
        P.op(eng, lambda e: e.tensor_tensor(out=out, in0=in0, in1=in1, op=op), r=r, w=w)

    def stt(out, in0, scalar, in1, op0, op1, r, w):
        P.op("dve", lambda e: e.scalar_tensor_tensor(out=out, in0=in0, scalar=scalar, in1=in1, op0=op0, op1=op1),
             r=r, w=w)

    def ts(eng, out, in0, s1, s2, op0, op1, r, w):
        if s2 is None:
            P.op(eng, lambda e: e.tensor_scalar(out=out, in0=in0, scalar1=s1, scalar2=None, op0=op0), r=r, w=w)
        else:
            P.op(eng, lambda e: e.tensor_scalar(out=out, in0=in0, scalar1=s1, scalar2=s2, op0=op0, op1=op1), r=r, w=w)

    def cp(eng, out, in_, r, w):
        if eng == "act":
            P.op("act", lambda e: e.copy(out=out, in_=in_), r=r, w=w)
        else:
            P.op(eng, lambda e: e.tensor_copy(out=out, in_=in_), r=r, w=w)

    def recip(out, in_, r, w):
        P.op("dve", lambda e: e.reciprocal(out=out, in_=in_), r=r, w=w)

    def memset(eng, ap, val, w):
        P.op(eng, lambda e: e.memset(ap, val), w=w)

    def dma(out, in_, key, r, w, eng="sp"):
        P.dma(eng, lambda e: e.dma_start(out=out, in_=in_), key=key, r=r, w=w)

    KT = lambda t: ("T", t)
    KPS = lambda b: ("ps", b)

    dma(CST[:, :], cst_d[:, :], key="cst", r=[], w=[("c", "cst")])
    memset("pool", ONESF[:, :], 1.0, w=[("c", "onesf")])
    dma(PERMF[:, :], perm_d[:, :], key="perm", r=[], w=[("c", "permf")])
    cp("pool", PERM[:, :], PERMF[:, :], r=[("c", "permf")], w=[("c", "perm")])
    memset("pool", ONESB[:, :], 1.0, w=[("c", "onesb")])

    _wst = [0]

    WK_ALL = ("wbf", "wbf1", "wbf2")

    def load_w(entries, keys=WK_ALL):
        for (src, c, c0) in entries:
            n = src.shape[-1]
            assert n <= 832
            s = _wst[0] % 2
            _wst[0] += 1
            st = WST[:, s * 832: s * 832 + n]
            dma(st, src, key=("wst", s), r=[], w=[("wst", s)])
            cp("pool", wb(c, c0, c0 + n), st, r=[("wst", s)], w=[(k_, c) for k_ in keys])

    def load_rope(tbl, t):
        dma(ROPE[:, 0:512], rope_d[2 * tbl, :, t * 512:(t + 1) * 512], key="ropeC", r=[], w=[("ropeC",)])
        dma(ROPE[:, 512:1024], rope_d[2 * tbl + 1, :, t * 512:(t + 1) * 512], key="ropeS", r=[], w=[("ropeS",)])

    ropeC = ROPE[:, 0:512]
    ropeS = ROPE[:, 512:1024]

    def proj_fm(bk, wcol0, M, t, nch=8, src=None, src_keys=None, wkey="wbf"):
        for c in range(nch):
            rhs = xt(c, t * 512, (t + 1) * 512) if src is None else src(c)
            mm(bank(bk)[0:M, :], wb(c, wcol0, wcol0 + M), rhs, c == 0, c == nch - 1,
               r=[(wkey, c), ("xT", t)], w=[KPS(bk)])

    def silu_to_og(bk, slab, t):
        act(og(slab, t * 512, (t + 1) * 512), bank(bk), AF.Silu, r=[KPS(bk)], w=[("og", slab, t)])

    def rstd_from_sq(src_ps_list, bss, lhs_const, lhs_key, inv_n, eps, out_t, nrows=128):
        for (b, ti) in src_ps_list:
            act(T(ti), bank(b), AF.Square, r=[KPS(b)], w=[KT(ti)])
        n = len(src_ps_list)
        for i, (b, ti) in enumerate(src_ps_list):
            mm(bank(bss), lhs_const, T(ti), i == 0, i == n - 1, r=[("c", lhs_key), KT(ti)], w=[KPS(bss)])
        act(T(out_t), bank(bss), AF.Ln, r=[KPS(bss)], w=[KT(out_t)], scale=inv_n, bias=eps)
        act(T(out_t), T(out_t), AF.Exp, r=[KT(out_t)], w=[KT(out_t)], scale=-0.5)

    def attention(kind, scale, slab, parity=0, lam_col=None, g_col=None, a_const=1.0):
        if kind == "A":
            groups = [[k] for k in range(NKT)]
            SG = [[0, 1], [2, 3]]
        else:
            groups = [list(range(g * 3, min(NKT, g * 3 + 3))) for g in range((NKT + 2) // 3)]
            SG = [[0, 1, 2], [3, 4, 5]]
        flat = [(q, gi, kts) for q in range(NTT) for gi, kts in enumerate(groups)]
        if kind == "A":
            rows = [(0, 64), (64, 128)]
        elif kind == "B":
            rows = [(parity * 64, parity * 64 + 64)]
        else:
            rows = [(0, 96)]

        sc_used = {}

        def scores(idx):
            q, gi, kts = flat[idx]
            sg = idx % 2
            used = []
            if kind == "A":
                k = kts[0]
                for c in range(2):
                    b = SG[sg][c]
                    a0, a1 = rows[c]
                    mm(bank(b), kT[a0:a1, k * 128:(k + 1) * 128], qT[a0:a1, q * 512:(q + 1) * 512], True, True,
                       r=[("kT", k // 4), ("qT", q)], w=[KPS(b)])
                    used.append(b)
            else:
                a0, a1 = rows[0]
                for i, k in enumerate(kts):
                    b = SG[sg][i]
                    rk = [("kT", k // 4), ("qT", q)]
                    if kind == "C":
                        rk.append(("kr", k // 4))
                        rk.append(("qTr", q))
                    mm(bank(b), kT[a0:a1, k * 128:(k + 1) * 128], qT[a0:a1, q * 512:(q + 1) * 512], True, True,
                       r=rk, w=[KPS(b)])
                    used.append(b)
            sc_used[idx] = used

        def scores_exp(idx):
            sg = idx % 2
            used = sc_used.pop(idx)
            n = len(used)
            b0 = used[0]
            act(pT(sg)[:, 0:n * 512], PS[:, b0 * 512:(b0 + n) * 512], AF.Exp,
                r=[KPS(b) for b in used], w=[("pT", sg)], scale=scale)

        def pv(idx):
            q, gi, kts = flat[idx]
            sg = idx % 2
            first = gi == 0
            last = gi == len(groups) - 1
            if kind == "A":
                k = kts[0]
                rr = [("V", k // 4), ("pT", sg)]
                mm(bank(4), V3[:, k, :], pT(sg)[:, 0:512], first, last, r=rr, w=[KPS(4)])
                mm(bank(6), V3[:, k, :], pT(sg)[:, 512:1024], first, last, r=rr, w=[KPS(6)])
                mm(bank(5)[0:64, :], ONESB[:, 0:64], pT(sg)[:, 0:512], first, last,
                   r=[("c", "onesb"), ("pT", sg)], w=[KPS(5)])
                mm(bank(5)[64:128, :], ONESB[:, 0:64], pT(sg)[:, 512:1024], first, last,
                   r=[("c", "onesb"), ("pT", sg)], w=[KPS(5)], tile_position=(0, 64))
            else:
                acc = 6 + (q % 2)
                for i, k in enumerate(kts):
                    mm(bank(acc)[0:96, :], V3[:, k, 0:96], pT(sg)[:, i * 512:(i + 1) * 512], first and i == 0,
                       last and i == len(kts) - 1, r=[("V", k // 4), ("Vones",), ("pT", sg)], w=[KPS(acc)])
            if last:
                epilogue(q)

        def epilogue(q):
            qs = (q * 512, (q + 1) * 512)
            if kind == "A":
                cp("dve", T(0), bank(4), r=[KPS(4)], w=[KT(0)])
                cp("dve", T(1), bank(6), r=[KPS(6)], w=[KT(1)])
                cp("dve", T(2), bank(5), r=[KPS(5)], w=[KT(2)])
                recip(T(2), T(2), r=[KT(2)], w=[KT(2)])
                lo_, hi_ = slice(0, 64), slice(64, 128)
                cp("dve", T(3)[lo_, :], T(2)[hi_, :], r=[KT(2)], w=[KT(3)])
                cp("dve", T(3)[hi_, :], T(2)[lo_, :], r=[KT(2)], w=[KT(3)])
                tt("dve", T(0)[lo_, :], T(0)[lo_, :], T(2)[lo_, :], ALU.mult, r=[KT(0), KT(2)], w=[KT(0)])
                tt("dve", T(0)[hi_, :], T(0)[hi_, :], T(3)[hi_, :], ALU.mult, r=[KT(0), KT(3)], w=[KT(0)])
                tt("dve", T(1)[lo_, :], T(1)[lo_, :], T(3)[lo_, :], ALU.mult, r=[KT(1), KT(3)], w=[KT(1)])
                tt("dve", T(1)[hi_, :], T(1)[hi_, :], T(2)[hi_, :], ALU.mult, r=[KT(1), KT(2)], w=[KT(1)])
                stt(T(0), T(1), lam_col, T(0), ALU.mult, ALU.add, r=[KT(0), KT(1), ("c", "lam")], w=[KT(0)])
                deferred.append([min(10, max(1, NKT // 2 - 1)), q, 0])
            else:
                acc = 6 + (q % 2)
                lo = slice(parity * 64, parity * 64 + 64)
                recip(T(2)[64:96, :], bank(acc)[64:96, :], r=[KPS(acc)], w=[KT(2)])
                p0 = parity * 64
                tt("dve", T(3)[p0:p0 + 32, :], bank(acc)[0:32, :], T(2)[64:96, :], ALU.mult,
                   r=[KPS(acc), KT(2)], w=[KT(3)])
                tt("dve", T(3)[p0 + 32:p0 + 64, :], bank(acc)[32:64, :], T(2)[64:96, :], ALU.mult,
                   r=[KPS(acc), KT(2)], w=[KT(3)])
                o = og(slab, qs[0], qs[1])[lo, :]
                tt("dve", o, T(3)[lo, :], o, ALU.mult, r=[KT(3), ("og", slab, q)], w=[("og", slab, q)])

        deferred = []

        def epi_stage(q, stage, b):
            qs = (q * 512, (q + 1) * 512)
            if stage == 0:
                act(T(1), T(0), AF.Square, r=[KT(0)], w=[KT(1)])
            elif stage == 1:
                b = 7
                mm(bank(b), ONESF[:, :], T(1), True, True, r=[("c", "onesf"), KT(1)], w=[KPS(b)])
                act(T(1), bank(b), AF.Ln, r=[KPS(b)], w=[KT(1)], scale=1.0 / 128.0, bias=RMS_EPS)
                act(T(1), T(1), AF.Exp, r=[KT(1)], w=[KT(1)], scale=-0.5)
            else:
                stt(T(0), T(0), g_col, T(1), ALU.mult, ALU.mult, r=[KT(0), KT(1), ("c", "cols")], w=[KT(0)])
                o = og(slab, qs[0], qs[1])
                stt(o, T(0), a_const, o, ALU.mult, ALU.mult, r=[KT(0), ("og", slab, q)], w=[("og", slab, q)])

        epi_bank = {}

        def run_deferred(idx, flush=False):
            while deferred:
                d = deferred[0]
                d[0] -= 1
                if d[0] > 0 and not flush:
                    break
                epi_stage(d[1], d[2], SG[(idx + 1) % 2][0])
                d[2] += 1
                if d[2] > 2:
                    deferred.pop(0)
                else:
                    d[0] = 2 if NKT >= 32 else 1
                if not flush:
                    break

        scores(0)
        scores_exp(0)
        if len(flat) > 1:
            scores(1)
            scores_exp(1)
        for idx in range(len(flat)):
            run_deferred(idx)
            if idx + 2 < len(flat):
                scores(idx + 2)
            pv(idx)
            if idx + 2 < len(flat):
                scores_exp(idx + 2)
        def finish():
            while deferred:
                run_deferred(0, flush=True)

        if kind == "A":
            return finish
        finish()
        return None

    def phase_T():
        for i in range(NKT):
            if i % 2 == 0:
                buf = TMP[:, 0:1024]; bkeys = [KT(0), KT(1)]
            else:
                buf = ARF[:, 0:1024]; bkeys = [("arf", 0)]
            dma(buf, x_d[i * 128:(i + 1) * 128, :], key=("xin", i % 2), r=[], w=bkeys)
            b0 = nbank(); b1 = nbank()
            for c in range(8):
                b = b0 if c < 4 else b1
                tr(bank(b)[:, (c % 4) * 128:(c % 4 + 1) * 128], buf[:, c * 128:(c + 1) * 128], r=bkeys, w=[KPS(b)])
            for hb, b in ((0, b0), (1, b1)):
                cp("dve", XT3[:, hb * 4:(hb + 1) * 4, i * 128:(i + 1) * 128],
                   bank(b).rearrange("p (c t) -> p c t", c=4), r=[KPS(b)], w=[("xT", i // 4)])

    WOUT_PREFETCH = (3 * S + 8192) <= 8 * S

    def prefetch_wout(L):
        if not WOUT_PREFETCH:
            return
        wout_d = wEo_d[L // 2] if L % 2 == 0 else wOo_d[L // 2]
        xk = [("xT", t) for t in range(NTT)]
        for c in range(8):
            for h in range(2):
                s = _wst[0] % 2
                _wst[0] += 1
                st = WST[:, s * 832: s * 832 + 512]
                dma(st, wout_d[:, c * 1024 + h * 512: c * 1024 + (h + 1) * 512], key=("wst", s), r=[],
                    w=[("wst", s)])
                o0 = 3 * S + c * 1024 + h * 512
                cp("pool", XT[:, o0:o0 + 512], st, r=[("wst", s)], w=xk)

    def phase_E(L, wout_d, ln_d, last, first):
        src_d = x_d if first else out_d
        qk_keys_lo = [("qT", t) for t in range(NTT)] + [("qTr", t) for t in range(NTT)]
        qk_keys_hi = [("kT", t) for t in range(NTT)] + [("kr", t) for t in range(NTT)]
        xt_keys = [("xT", t) for t in range(NTT)]
        if WOUT_PREFETCH:
            for q4 in range(4):
                eng = "dve" if q4 % 2 == 0 else "act"
                cp(eng, QK[:, q4 * 2048:(q4 + 1) * 2048], XT[:, 3 * S + q4 * 2048: 3 * S + (q4 + 1) * 2048],
                   r=xt_keys, w=(qk_keys_lo if q4 < 2 else qk_keys_hi))
        else:
            for c in range(8):
                for h in range(2):
                    s = _wst[0] % 2
                    _wst[0] += 1
                    st = WST[:, s * 832: s * 832 + 512]
                    dma(st, wout_d[:, c * 1024 + h * 512: c * 1024 + (h + 1) * 512], key=("wst", s), r=[],
                        w=[("wst", s)])
                    cp("pool", QK[:, c * 1024 + h * 512: c * 1024 + (h + 1) * 512], st, r=[("wst", s)],
                       w=(qk_keys_lo if c < 4 else qk_keys_hi))
        dma(LNP[:, :], ln_d[:, :], key="lnp", r=[], w=[("c", "lnp")])
        lng = LNP[:, 0:1024]
        lnb = LNP[:, 1024:2048]
        junk = ROPE[:, 0:1024]
        sets = [
            (TMP[:, 0:1024], TMP[:, 1024:2048], [KT(0), KT(1)], [KT(2), KT(3)]),
            (ARF[:, 0:1024], ARF[:, 1024:2048], [("arf", 0)], [("arf", 1)]),
            (TMP[:, 2048:3072], ARF[:, 2048:3072], [KT(4), KT(5)], [("arf", 2)]),
        ]
        def ctx(i):
            si = i % 3
            xr, xn, kx, kn = sets[si]
            return slice(i * 128, (i + 1) * 128), xr, xn, kx, kn, SM[:, si * 8: si * 8 + 8], ("sm", si), si

        def ybanks(i):
            return 2 * (i % 3), 2 * (i % 3) + 1

        def e_y(i):
            b0, b1 = ybanks(i)
            for h, b in ((0, b0), (1, b1)):
                for c in range(8):
                    mm(bank(b), og(c, i * 128, (i + 1) * 128), QK[:, c * 1024 + h * 512: c * 1024 + (h + 1) * 512],
                       c == 0, c == 7, r=[("og", c, i // 4)] + (qk_keys_lo if c < 4 else qk_keys_hi), w=[KPS(b)])

        def e_ld(i):
            ts_, xr, xn, kx, kn, sm, ksm, si = ctx(i)
            dma(xr, src_d[ts_, :], key=("xin", si), r=[("xhbm", i)], w=kx)

        def e_r(i):
            ts_, xr, xn, kx, kn, sm, ksm, si = ctx(i)
            b0, b1 = ybanks(i)
            for h, b in ((0, b0), (1, b1)):
                stt(xr[:, h * 512:(h + 1) * 512], xr[:, h * 512:(h + 1) * 512], float(ALPHA), bank(b),
                    ALU.mult, ALU.add, r=kx + [KPS(b)], w=kx)

        def e_stats(i):
            ts_, xr, xn, kx, kn, sm, ksm, si = ctx(i)
            act(junk, xr, AF.Identity, r=kx, w=[ksm, ("junk",)], accum_out=sm[:, 0:1])
            act(junk, xr, AF.Square, r=kx, w=[ksm, ("junk",)], accum_out=sm[:, 1:2])

        def e_smalls(i):
            ts_, xr, xn, kx, kn, sm, ksm, si = ctx(i)
            ts("dve", sm[:, 2:3], sm[:, 0:1], -1.0 / D, None, ALU.mult, None, r=[ksm], w=[ksm])
            tt("dve", sm[:, 3:4], sm[:, 2:3], sm[:, 2:3], ALU.mult, r=[ksm], w=[ksm])
            stt(sm[:, 4:5], sm[:, 1:2], 1.0 / D, sm[:, 3:4], ALU.mult, ALU.subtract, r=[ksm], w=[ksm])

        def e_lnexp(i):
            ts_, xr, xn, kx, kn, sm, ksm, si = ctx(i)
            act(sm[:, 5:6], sm[:, 4:5], AF.Ln, r=[ksm], w=[ksm], bias=LN_EPS)
            act(sm[:, 5:6], sm[:, 5:6], AF.Exp, r=[ksm], w=[ksm], scale=-0.5)
            act(sm[:, 6:7], sm[:, 2:3], AF.Identity, r=[ksm], w=[ksm], scale=sm[:, 5:6])

        def e_nmr(i):
            return

        def e_xn(i):
            ts_, xr, xn, kx, kn, sm, ksm, si = ctx(i)
            act(xn, xr, AF.Identity, r=kx + [ksm], w=kn, scale=sm[:, 5:6], bias=sm[:, 6:7])

        def e_gb(i):
            ts_, xr, xn, kx, kn, sm, ksm, si = ctx(i)
            tt("dve", xn, xn, lng, ALU.mult, r=kn + [("c", "lnp")], w=kn)
            tt("dve", xn, xn, lnb, ALU.add, r=kn + [("c", "lnp")], w=kn)
            dma(out_d[ts_, :], xn, key=("xout", si), r=kn, w=[("xhbm", i)], eng="pool")

        def e_tr(i):
            if last:
                return
            ts_, xr, xn, kx, kn, sm, ksm, si = ctx(i)
            for c in range(8):
                b = 6 if c < 4 else 7
                tr(bank(b)[:, (c % 4) * 128:(c % 4 + 1) * 128], xn[:, c * 128:(c + 1) * 128], r=kn, w=[KPS(b)])

        def e_casts(i):
            if last:
                return
            for hb, b in ((0, 6), (1, 7)):
                cp("act", XT3[:, hb * 4:(hb + 1) * 4, i * 128:(i + 1) * 128],
                   bank(b).rearrange("p (c t) -> p c t", c=4), r=[KPS(b)], w=[("xT", i // 4)])

        plan = [(e_y, 0), (e_smalls, 3), (e_lnexp, 3), (e_gb, 4), (e_tr, 4), (e_nmr, 3), (e_xn, 3),
                (e_r, 2), (e_stats, 2), (e_casts, 4), (e_ld, 0)]
        for k in range(NKT + 4):
            for fn, off in plan:
                i = k - off
                if 0 <= i < NKT:
                    fn(i)

    def w_entries(d_ap, ncols, nch=8, c_base=0, col0=0):
        return [(d_ap[:, c * ncols:(c + 1) * ncols], c_base + c, col0) for c in range(nch)]

    def even_first_weights(i):
        load_w(w_entries(wA_d[i, 0], 512))

    def rope_combine(bq, bqs, out_ap, out_keys, rows=slice(0, 128), tp=(0, 1)):
        ta, tb = tp
        tt("dve", T(ta)[rows, :], bank(bq)[rows, :], ropeC[rows, :], ALU.mult, r=[KPS(bq), ("ropeC",)], w=[KT(ta)])
        tt("dve", T(tb)[rows, :], bank(bqs)[rows, :], ropeS[rows, :], ALU.mult, r=[KPS(bqs), ("ropeS",)], w=[KT(tb)])
        tt("pool", out_ap, T(ta)[rows, :], T(tb)[rows, :], ALU.add, r=[KT(ta), KT(tb)], w=out_keys)

    def rms_rope(bq, bqs, gcol, gswcol, out_ap, out_keys):
        bss = nbank()
        rstd_from_sq([(bq, 2)], bss, blk, "cst", 1.0 / 64.0, RMS_EPS, 3)
        stt(T(0), bank(bq), gcol, ropeC, ALU.mult, ALU.mult, r=[KPS(bq), ("ropeC",), ("c", "cols")], w=[KT(0)])
        stt(T(1), bank(bqs), gswcol, ropeS, ALU.mult, ALU.mult, r=[KPS(bqs), ("ropeS",), ("c", "cols")], w=[KT(1)])
        tt("pool", T(0), T(0), T(1), ALU.add, r=[KT(0), KT(1)], w=[KT(0)])
        tt("pool", out_ap, T(0), T(3), ALU.mult, r=[KT(0), KT(3)], w=out_keys)

    def v_proj(t, wcol0, dv, nch=8, src=None):
        bv = nbank()
        for j in range(4):
            tok = (t * 4 + j) * 128
            for c in range(nch):
                lhs = xt(c, tok, tok + 128) if src is None else src(c, tok)
                mm(bank(bv)[:, j * dv:(j + 1) * dv], lhs, wb(c, wcol0, wcol0 + dv), c == 0, c == nch - 1,
                   r=[("wbf", c), ("xT", t)], w=[KPS(bv)])
        cp("act", V3[:, t * 4:(t + 1) * 4, 0:dv], bank(bv)[:, 0:4 * dv].rearrange("p (k d) -> p k d", k=4),
           r=[KPS(bv)], w=[("V", t)])

    def QB(slot, sub=0):
        return pT(slot)[:, sub * 512:(sub + 1) * 512]

    def a_stage1(t, col0, slot):
        b = nbank()
        proj_fm(b, col0, 128, t)
        cp("act", QB(slot, t % 2), bank(b), r=[KPS(b)], w=[("pT", slot)])
        return b

    def a_stage2(t, b, slot, out_ap, out_keys, tp):
        bs = nbank()
        mm(bank(bs), PERM[:, 0:128], QB(slot, t % 2), True, True, r=[("c", "perm"), ("pT", slot)], w=[KPS(bs)])
        rope_combine(b, bs, out_ap, out_keys, tp=tp)

    def b_stage1(t, col0, slot, gcol, wkey="wbf"):
        b = nbank()
        proj_fm(b, col0, 128, t, wkey=wkey)
        act(QB(slot), bank(b), AF.Copy, r=[KPS(b), ("c", "cols")], w=[("pT", slot)], scale=gcol)
        act(T(2 + 2 * slot), bank(b), AF.Square, r=[KPS(b)], w=[KT(2 + 2 * slot)])
        return b

    def b_stage2(t, b, slot, gcol, out_ap, out_keys):
        bs = nbank(); bss = nbank()
        tsq = 2 + 2 * slot
        trs = 3 + 2 * slot
        mm(bank(bs), PERM[:, 128:256], QB(slot), True, True, r=[("c", "perm"), ("pT", slot)], w=[KPS(bs)])
        mm(bank(bss), blk, T(tsq), True, True, r=[("c", "cst"), KT(tsq)], w=[KPS(bss)])
        act(T(trs), bank(bss), AF.Ln, r=[KPS(bss)], w=[KT(trs)], scale=1.0 / 64.0, bias=RMS_EPS)
        act(T(trs), T(trs), AF.Exp, r=[KT(trs)], w=[KT(trs)], scale=-0.5)
        stt(T(0), bank(b), gcol, ropeC, ALU.mult, ALU.mult, r=[KPS(b), ("ropeC",), ("c", "cols")], w=[KT(0)])
        tt("dve", T(1), bank(bs), ropeS, ALU.mult, r=[KPS(bs), ("ropeS",)], w=[KT(1)])
        tt("pool", T(0), T(0), T(1), ALU.add, r=[KT(0), KT(1)], w=[KT(0)])
        tt("pool", out_ap, T(0), T(trs), ALU.mult, r=[KT(0), KT(trs)], w=out_keys)

    def even_layer(L, nxt_loader):
        i = L // 2
        lam_init = 0.8 - 0.6 * math.exp(-0.3 * L)
        dma(COLS[:, :], colsE_d[i], key="cols", r=[], w=[("c", "cols")])
        dma(LAM[:, :], lamE_d[i], key="lam", r=[], w=[("c", "lamraw")])
        tt("dve", LAM[:, 0:64], LAM[:, 0:64], LAM[:, 64:128], ALU.mult, r=[("c", "lamraw")], w=[("c", "lamraw")])
        tt("dve", LAM[:, 128:192], LAM[:, 128:192], LAM[:, 192:256], ALU.mult, r=[("c", "lamraw")], w=[("c", "lamraw")])
        P.op("dve", lambda e: e.reduce_sum(out=SM[:, 16:17], in_=LAM[:, 0:64], axis=mybir.AxisListType.X),
             r=[("c", "lamraw")], w=[("sm", 2)])
        P.op("dve", lambda e: e.reduce_sum(out=SM[:, 17:18], in_=LAM[:, 128:192], axis=mybir.AxisListType.X),
             r=[("c", "lamraw")], w=[("sm", 2)])
        act(SM[:, 16:18], SM[:, 16:18], AF.Exp, r=[("sm", 2)], w=[("sm", 2)])
        tt("dve", SM[:, 18:19], SM[:, 17:18], SM[:, 16:17], ALU.subtract, r=[("sm", 2)], w=[("sm", 2)])
        ts("dve", SM[:, 19:20], SM[:, 18:19], -lam_init, None, ALU.add, None, r=[("sm", 2)], w=[("c", "lam")])
        neglam = SM[:, 19:20]

        pend = None
        for h in range(4):
            for t in range(NTT):
                bg = nbank()
                proj_fm(bg, 384, 128, t)
                silu_to_og(bg, h, t)
            if pend is not None:
                pend()
                pend = None
            prev = None
            for t in range(NTT + 1):
                cur = None
                if t < NTT:
                    cur = (a_stage1(t, 0, 0), a_stage1(t, 128, 1))
                if prev is not None:
                    tp_ = t - 1
                    load_rope(0, tp_)
                    a_stage2(tp_, prev[0], 0, qT[:, tp_ * 512:(tp_ + 1) * 512], [("qT", tp_), ("qTr", tp_)], (0, 1))
                    a_stage2(tp_, prev[1], 1, kT[:, tp_ * 512:(tp_ + 1) * 512], [("kT", tp_), ("kr", tp_)], (4, 5))
                    v_proj(tp_, 256, 128)
                prev = cur
            if h < 3:
                load_w(w_entries(wA_d[i, h + 1], 512))
            else:
                load_w(w_entries(wBq_d[i, 0], 256, col0=192), keys=("wbf", "wbf1", "wbf2"))
                load_w(w_entries(wBq_d[i, 1], 256, col0=448), keys=("wbf", "wbf2"))
                load_w(w_entries(wBk_d[i, 0], 192), keys=("wbf",))
            pend = attention("A", 0.125, h, lam_col=neglam, g_col=COLS[:, 0:1], a_const=float(1.0 - lam_init))
        pend()

        memset("pool", V3[:, :, 64:128], 1.0, w=[("Vones",)] + [("V", t) for t in range(NTT)])
        for g in range(2):
            prev = None
            for t in range(NTT + 1):
                cur = None
                if t < NTT:
                    cur = b_stage1(t, 0, t % 2, COLS[:, 3:4])
                if prev is not None:
                    tp_ = t - 1
                    load_rope(1, tp_)
                    b_stage2(tp_, prev, tp_ % 2, COLS[:, 3:4], kT[:, tp_ * 512:(tp_ + 1) * 512],
                             [("kT", tp_), ("kr", tp_)])
                    v_proj(tp_, 128, 64)
                prev = cur
            for pp in range(2):
                slab = 4 + g * 2 + pp
                wk_ = "wbf1" if pp == 0 else "wbf2"
                cb_ = 192 + pp * 256
                for t in range(NTT):
                    bg = nbank()
                    proj_fm(bg, cb_ + 128, 128, t, wkey=wk_)
                    silu_to_og(bg, slab, t)
                prev = None
                for t in range(NTT + 1):
                    cur = None
                    if t < NTT:
                        cur = b_stage1(t, cb_, t % 2, COLS[:, 1:2], wkey=wk_)
                    if prev is not None:
                        tp_ = t - 1
                        load_rope(1, tp_)
                        b_stage2(tp_, prev, tp_ % 2, COLS[:, 1:2], qT[:, tp_ * 512:(tp_ + 1) * 512],
                                 [("qT", tp_), ("qTr", tp_)])
                    prev = cur
                if pp == 0:
                    pass
                elif g == 0:
                    load_w(w_entries(wBq_d[i, 2], 256, col0=192), keys=("wbf1",))
                    load_w(w_entries(wBq_d[i, 3], 256, col0=448), keys=("wbf2",))
                    load_w(w_entries(wBk_d[i, 1], 192), keys=("wbf",))
                else:
                    nxt_loader()
                    prefetch_wout(L)
                for jj in range(2):
                    attention("B", 0.125, slab, parity=jj)

    def odd_first_weights(i):
        load_w([(wG1_d[i][:, c * 768: c * 768 + 384], c, 0) for c in range(8)])
        load_w([(wG1_d[i][:, c * 768 + 384: c * 768 + 768], c, 448) for c in range(8)])

    def ropeCp(t):
        return LNP[(t % 4) * 32:(t % 4) * 32 + 32, (t // 4) * 512:(t // 4 + 1) * 512]

    def ropeSp(t):
        return LNP[(t % 4) * 32:(t % 4) * 32 + 32, 1024 + (t // 4) * 512:1024 + (t // 4 + 1) * 512]

    def load_ropep():
        for t in range(NTT):
            dma(ropeCp(t), rope_d[4, 64:96, t * 512:(t + 1) * 512], key="ropep", r=[], w=[("c", "ropep")])
            dma(ropeSp(t), rope_d[5, 64:96, t * 512:(t + 1) * 512], key="ropep", r=[], w=[("c", "ropep")])

    def odd_layer(L, nxt_loader):
        i = L // 2
        load_ropep()
        dma(COLS[:, :], colsO_d[i], key="cols", r=[], w=[("c", "cols")])
        for t in range(NTT):
            for s_ in range(3):
                bg = nbank()
                proj_fm(bg, s_ * 128, 128, t, wkey="wbf")
                silu_to_og(bg, s_, t)
        load_w([(wG2_d[i][:, c * 832: c * 832 + 256], c, 0) for c in range(8)], keys=("wbf",))
        load_w([(wG2_d[i][:, c * 832 + 640: c * 832 + 832], c, 256) for c in range(8)], keys=("wbf",))
        memset("pool", V3[:, :, 64:128], 1.0, w=[("Vones",)] + [("V", t) for t in range(NTT)])
        for t in range(NTT):
            for s_ in range(3):
                bg = nbank()
                proj_fm(bg, 448 + s_ * 128, 128, t, wkey="wbf2")
                silu_to_og(bg, 3 + s_, t)
        load_w([(wG2_d[i][:, c * 832 + 256: c * 832 + 640], c, 448) for c in range(8)], keys=("wbf2",))
        for t in range(NTT):
            for s_ in range(2):
                bg = nbank()
                proj_fm(bg, s_ * 128, 128, t, wkey="wbf")
                silu_to_og(bg, 6 + s_, t)
        for t in range(NTT):
            bkr = nbank(); bkrs = nbank()
            proj_fm(bkr, 256, 96, t, wkey="wbf"); proj_fm(bkrs, 352, 96, t, wkey="wbf")
            r64 = slice(64, 96)
            tt("dve", T(0)[r64, :], bank(bkr)[r64, :], ropeCp(t), ALU.mult, r=[KPS(bkr), ("c", "ropep")], w=[KT(0)])
            tt("dve", T(1)[r64, :], bank(bkrs)[r64, :], ropeSp(t), ALU.mult, r=[KPS(bkrs), ("c", "ropep")], w=[KT(1)])
            tt("pool", kT[64:96, t * 512:(t + 1) * 512], T(0)[r64, :], T(1)[r64, :], ALU.add, r=[KT(0), KT(1)],
               w=[("kr", t)])
        for t in range(NTT):
            bc0 = nbank(); bc1 = nbank(); bkv = nbank()
            proj_fm(bc0, 448, 128, t, wkey="wbf2"); proj_fm(bc1, 576, 128, t, wkey="wbf2")
            proj_fm(bkv, 704, 128, t, wkey="wbf2")
            bss = nbank()
            rstd_from_sq([(bc0, 2), (bc1, 3)], bss, ONESF[:, :], "onesf", 1.0 / 256.0, RMS_EPS, 5)
            stt(xt(0, t * 512, (t + 1) * 512), bank(bc0), COLS[:, 0:1], T(5), ALU.mult, ALU.mult,
                r=[KPS(bc0), KT(5), ("c", "cols")], w=[("xT", t)])
            stt(xt(1, t * 512, (t + 1) * 512), bank(bc1), COLS[:, 1:2], T(5), ALU.mult, ALU.mult,
                r=[KPS(bc1), KT(5), ("c", "cols")], w=[("xT", t)])
            bss2 = nbank()
            rstd_from_sq([(bkv, 2)], bss2, ONESF[:, :], "onesf", 1.0 / 128.0, RMS_EPS, 3)
            stt(xt(2, t * 512, (t + 1) * 512), bank(bkv), COLS[:, 2:3], T(3), ALU.mult, ALU.mult,
                r=[KPS(bkv), KT(3), ("c", "cols")], w=[("xT", t)])

        def c_weights(j, keys=WK_ALL):
            ent = [(wQb_d[i, j][:, c * 192:(c + 1) * 192], c, 0) for c in range(2)]
            ent.append((wKvb_d[i][:, j * 128:(j + 1) * 128], 2, 0))
            load_w(ent, keys=keys)

        c_weights(0, keys=("wbf",))
        sc = 96.0 ** -0.5
        for j in range(16):
            for t in range(NTT):
                bq = nbank(); bqs = nbank()
                proj_fm(bq, 0, 96, t, nch=2); proj_fm(bqs, 96, 96, t, nch=2)
                r64 = slice(64, 96)
                tsl = slice(t * 512, (t + 1) * 512)
                ta, tb = (0, 1) if t % 2 == 0 else (4, 5)
                tt("dve", T(ta)[r64, :], bank(bq)[r64, :], ropeCp(t), ALU.mult, r=[KPS(bq), ("c", "ropep")], w=[KT(ta)])
                tt("dve", T(tb)[r64, :], bank(bqs)[r64, :], ropeSp(t), ALU.mult, r=[KPS(bqs), ("c", "ropep")], w=[KT(tb)])
                cp("act", qT[0:64, tsl], bank(bq)[0:64, :], r=[KPS(bq)], w=[("qT", t)])
                tt("pool", qT[64:96, tsl], T(ta)[r64, :], T(tb)[r64, :], ALU.add, r=[KT(ta), KT(tb)], w=[("qTr", t)])
                bk = nbank()
                mm(bank(bk)[0:64, :], wb(2, 0, 64), xt(2, t * 512, (t + 1) * 512), True, True,
                   r=[("wbf", 2), ("xT", t)], w=[KPS(bk)])
                cp("act", kT[0:64, t * 512:(t + 1) * 512], bank(bk)[0:64, :], r=[KPS(bk)], w=[("kT", t)])
                bv = nbank()
                for jj in range(4):
                    tok = (t * 4 + jj) * 128
                    mm(bank(bv)[:, jj * 64:(jj + 1) * 64], xt(2, tok, tok + 128), wb(2, 64, 128), True, True,
                       r=[("wbf", 2), ("xT", t)], w=[KPS(bv)])
                cp("act", V3[:, t * 4:(t + 1) * 4, 0:64], bank(bv)[:, 0:256].rearrange("p (k d) -> p k d", k=4),
                   r=[KPS(bv)], w=[("V", t)])
            if j < 15:
                c_weights(j + 1)
            else:
                nxt_loader()
                prefetch_wout(L)
            attention("C", sc, j // 2, parity=j % 2)

    def first_weights(L):
        if L % 2 == 0:
            even_first_weights(L // 2)
        else:
            odd_first_weights(L // 2)

    if from_x and layers[0] == 0:
        phase_T()
    else:
        phase_T()
    first_weights(layers[0])
    P.barrier()
    for li, L in enumerate(layers):
        last = li == len(layers) - 1
        nxt = (lambda L2=layers[li + 1]: first_weights(L2)) if not last else (lambda: None)
        if L % 2 == 0:
            even_layer(L, nxt)
        else:
            odd_layer(L, nxt)
        P.barrier()
        if L % 2 == 0:
            phase_E(L, wEo_d[L // 2], lnE_d[L // 2], last, li == 0)
        else:
            phase_E(L, wOo_d[L // 2], lnO_d[L // 2], last, li == 0)
        P.barrier()

    P.finalize(nc, es)
    with nc.Block() as block:
        @block.tensor
        def _(e):
            P.emit("pe", e)

        @block.scalar
        def _(e):
            P.emit("act", e)

        @block.vector
        def _(e):
            P.emit("dve", e)

        @block.gpsimd
        def _(e):
            P.emit("pool", e)

        @block.sync
        def _(e):
            P.emit("sp", e, final_keys=[("xout", 0), ("xout", 1), ("xout", 2)])
    es.close()
    return nc, P


def _tile_w(w, ncols):
    k = w.shape[0]
    return np.ascontiguousarray(w.reshape(k // 128, 128, ncols).transpose(1, 0, 2).reshape(128, -1))


def _gather_cols(w, idx):
    idx = np.asarray(idx)
    o = np.zeros((w.shape[0], len(idx)), np.float32)
    m = idx >= 0
    o[:, m] = w[:, idx[m]]
    return o


def _partner(d, half):
    return d + half if (d % (2 * half)) < half else d - half


def _rope_tables():
    pos = np.arange(S, dtype=np.float32)
    row = (np.arange(S) // 64).astype(np.float32)
    col = (np.arange(S) % 64).astype(np.float32)

    def angles(p, dims, theta):
        inv = np.float32(theta) ** (-(np.arange(0, dims, 2, dtype=np.float32) / np.float32(dims)))
        inv = inv.astype(np.float32)
        ang = (p[None, :] * inv[:, None]).astype(np.float32)
        return np.cos(ang).astype(np.float32), np.sin(ang).astype(np.float32)

    tabs = np.zeros((6, 128, S), np.float32)
    tabs[0::2] = 1.0
    ca, sa = angles(pos, 16, 500000.0)
    for p in range(128):
        d = p % 64
        if d < 16:
            f = d % 8
            tabs[0, p] = ca[f]
            tabs[1, p] = -sa[f] if d < 8 else sa[f]
    cr, sr = angles(row, 32, 10000.0)
    cc, sc = angles(col, 32, 10000.0)
    for p in range(128):
        d = p % 64
        if d < 32:
            f = d % 16
            tabs[2, p] = cr[f]
            tabs[3, p] = -sr[f] if d < 16 else sr[f]
        else:
            dd = d - 32
            f = dd % 16
            tabs[2, p] = cc[f]
            tabs[3, p] = -sc[f] if dd < 16 else sc[f]
    c3, s3 = angles(pos, 32, 500000.0)
    for p in range(64, 96):
        dd = p - 64
        f = dd % 16
        tabs[4, p] = c3[f]
        tabs[5, p] = -s3[f] if dd < 16 else s3[f]
    return tabs


def _prep_weights(inp):
    f = lambda a: np.asarray(a, dtype=np.float32)
    ev_w_in, ev_w_out = f(inp["ev_w_in"]), f(inp["ev_w_out"])
    od_w_in, od_w_out = f(inp["od_w_in"]), f(inp["od_w_out"])
    od_w_qb, od_w_kvb = f(inp["od_w_qb"]), f(inp["od_w_kvb"])
    wA = np.zeros((2, 4, 128, 8 * 512), np.float32)
    wBk = np.zeros((2, 2, 128, 8 * 192), np.float32)
    wBq = np.zeros((2, 4, 128, 8 * 256), np.float32)
    wEo = np.zeros((2, 128, 8 * 1024), np.float32)
    colsE = np.zeros((2, 128, 8), np.float32)
    lamE = np.zeros((2, 128, 256), np.float32)
    lnE = np.zeros((2, 128, 2048), np.float32)
    pa = np.array([(_partner(d, 8) if d < 16 else -1) for d in range(64)])
    pb = np.array([_partner(d, 16) for d in range(64)])
    for i in range(2):
        W = ev_w_in[i]
        for h in range(4):
            q = np.arange(128) + h * 128
            qsw = np.array([(h * 128 + (p // 64) * 64 + pa[p % 64]) if pa[p % 64] >= 0 else -1 for p in range(128)])
            k = q + 512
            ksw = np.where(qsw >= 0, qsw + 512, -1)
            v = np.arange(128) + 1024 + h * 128
            g = np.arange(128) + 2304 + h * 128
            idx = np.concatenate([q, k, v, g])
            wA[i, h] = _tile_w(_gather_cols(W, idx), 512)
        for g_ in range(2):
            k = 2048 + g_ * 64 + np.arange(64)
            ksw = 2048 + g_ * 64 + pb
            v = 2176 + g_ * 64 + np.arange(64)
            idx = np.concatenate([k, k, v])
            wBk[i, g_] = _tile_w(_gather_cols(W, idx), 192)
            for pp in range(2):
                j0 = g_ * 4 + pp * 2
                q = 1536 + j0 * 64 + np.arange(128)
                qsw = np.array([1536 + (j0 + p // 64) * 64 + pb[p % 64] for p in range(128)])
                gt = 2304 + 512 + j0 * 64 + np.arange(128)
                idx = np.concatenate([q, gt])
                wBq[i, g_ * 2 + pp] = _tile_w(_gather_cols(W, idx), 256)
        wEo[i] = _tile_w(ev_w_out[i], 1024)
        qn, kn = f(inp["ev_qnorm"])[i], f(inp["ev_knorm"])[i]
        colsE[i, :, 0] = f(inp["ev_subln"])[i]
        colsE[i, :, 1] = np.tile(qn, 2)
        colsE[i, :, 2] = np.tile(qn[pb], 2)
        colsE[i, :, 3] = np.tile(kn, 2)
        colsE[i, :, 4] = np.tile(kn[pb], 2)
        lamE[i] = np.broadcast_to(f(inp["ev_lam"])[i].reshape(1, 256), (128, 256))
        lnE[i, :, 0:1024] = np.broadcast_to(f(inp["ev_ln_g"])[i][None, :], (128, 1024))
        lnE[i, :, 1024:2048] = np.broadcast_to(f(inp["ev_ln_b"])[i][None, :], (128, 1024))
    wG1 = np.zeros((2, 128, 8 * 768), np.float32)
    wG2 = np.zeros((2, 128, 8 * 832), np.float32)
    wQb = np.zeros((2, 16, 128, 2 * 192), np.float32)
    wKvb = np.zeros((2, 128, 2048), np.float32)
    wOo = np.zeros((2, 128, 8 * 1024), np.float32)
    colsO = np.zeros((2, 128, 8), np.float32)
    lnO = np.zeros((2, 128, 2048), np.float32)
    pc = np.array([_partner(d, 16) for d in range(32)])
    for i in range(2):
        W = od_w_in[i]
        wG1[i] = _tile_w(_gather_cols(W, 416 + np.arange(768)), 768)
        kr = np.concatenate([-np.ones(64, int), 384 + np.arange(32)])
        krsw = np.concatenate([-np.ones(64, int), 384 + pc])
        idx = np.concatenate([416 + 768 + np.arange(256), np.arange(256), 256 + np.arange(128), kr, krsw])
        wG2[i] = _tile_w(_gather_cols(W, idx), 832)
        for j in range(16):
            q = j * 96 + np.arange(96)
            qsw = np.concatenate([-np.ones(64, int), j * 96 + 64 + pc])
            wQb[i, j] = _tile_w(_gather_cols(od_w_qb[i], np.concatenate([q, qsw])), 192)
        wKvb[i] = od_w_kvb[i]
        wOo[i] = _tile_w(od_w_out[i], 1024)
        qn = f(inp["od_qnorm"])[i]
        colsO[i, :, 0] = qn[0:128]
        colsO[i, :, 1] = qn[128:256]
        colsO[i, :, 2] = f(inp["od_kvnorm"])[i]
        lnO[i, :, 0:1024] = np.broadcast_to(f(inp["od_ln_g"])[i][None, :], (128, 1024))
        lnO[i, :, 1024:2048] = np.broadcast_to(f(inp["od_ln_b"])[i][None, :], (128, 1024))
    perm = np.zeros((128, 256), np.float32)
    for p in range(128):
        d = p % 64
        if pa[d] >= 0:
            perm[(p // 64) * 64 + pa[d], p] = 1.0
        perm[(p // 64) * 64 + pb[d], 128 + p] = 1.0
    cst = np.zeros((128, 256), np.float32)
    cst[:, 0:128] = np.eye(128, dtype=np.float32)
    cst[0:64, 128:192] = 1.0
    cst[64:128, 192:256] = 1.0
    return dict(wA=wA, wBk=wBk, wBq=wBq, wEo=wEo, wG1=wG1, wG2=wG2, wQb=wQb, wKvb=wKvb, wOo=wOo,
                colsE=colsE, lamE=lamE, lnE=lnE, colsO=colsO, lnO=lnO, rope=_rope_tables(), cst=cst, perm=perm)


_CACHE = {}


def run_layers(x_full, weights, layers, cores=None, trace=False):
    key = tuple(layers)
    if key not in _CACHE:
        _CACHE[key] = build_program(list(layers))
    nc, _ = _CACHE[key]
    n = x_full.shape[0]
    in_maps = []
    for b in range(n):
        m = dict(weights)
        m["x"] = np.ascontiguousarray(x_full[b], dtype=np.float32)
        in_maps.append(m)
    res = run_bass_kernel_spmd(nc, in_maps, core_ids=list(range(n)), **({"trace": True} if trace else {}))
    return np.stack([r["out"] for r in res.results], axis=0), res


def kernel(**inputs):
    x = np.asarray(inputs["x"], dtype=np.float32)
    weights = _prep_weights(inputs)
    out, _ = run_layers(x, weights, (0, 1, 2, 3))
    return out.astype(np.float32)
```

```python
import math
import numpy as np
from contextlib import ExitStack
import concourse.bass as bass
import concourse.mybir as mybir
from concourse.bass_utils import run_bass_kernel_spmd

F32 = mybir.dt.float32
BF16 = mybir.dt.bfloat16
AF = mybir.ActivationFunctionType
ALU = mybir.AluOpType

S = 4096
D = 1024
NTT = 8
NKT = 32
DEPTH = 4
ALPHA = (2 * DEPTH) ** 0.25
LN_EPS = 1e-5
RMS_EPS = 1e-6
WCOLS = 832


class Op:
    __slots__ = ("eng", "fn", "deps", "kind", "key", "cnt", "sig", "need")


class Prog:
    ENGS = ("pe", "act", "dve", "pool", "sp")

    def __init__(self):
        self.ops = {e: [] for e in self.ENGS}
        self.lastw = {}
        self.readers = {}
        self.dcnt = {}
        self.last_dma = {}
        self.pending = {e: set() for e in self.ENGS}

    def _add(self, op, r, w):
        w = list(w) + [k for k in r if k[0] == "ps"]
        r = [k for k in r if k[0] != "ps"]
        deps = set()
        for k in r:
            o = self.lastw.get(k)
            if o is not None:
                deps.add(o)
        for k in w:
            o = self.lastw.get(k)
            if o is not None:
                deps.add(o)
            rd = self.readers.get(k)
            if rd:
                deps.update(rd.values())
        if self.pending[op.eng]:
            deps |= self.pending[op.eng]
            self.pending[op.eng] = set()
        deps.discard(op)
        op.deps = deps
        rk = (op.eng if op.kind == "c" else ("d", op.key))
        for k in r:
            self.readers.setdefault(k, {})[rk] = op
        for k in w:
            self.lastw[k] = op
            self.readers[k] = {}
        self.ops[op.eng].append(op)
        return op

    def op(self, eng, fn, r=(), w=()):
        o = Op()
        o.eng = eng; o.fn = fn; o.kind = "c"; o.key = None; o.cnt = 0; o.sig = 0; o.need = False
        return self._add(o, r, w)

    def dma(self, eng, fn, key, r=(), w=()):
        o = Op()
        o.eng = eng; o.fn = fn; o.kind = "d"; o.key = key; o.sig = 0; o.need = True
        self.dcnt[key] = self.dcnt.get(key, 0) + 16
        o.cnt = self.dcnt[key]
        self.last_dma[key] = o
        return self._add(o, r, w)

    def barrier(self):
        deps = set()
        for e in self.ENGS:
            for o in reversed(self.ops[e]):
                if o.kind == "c":
                    deps.add(o)
                    break
        for o in self.last_dma.values():
            deps.add(o)
        for e in self.ENGS:
            self.pending[e] = set(deps)

    def finalize(self, nc, es):
        for e in self.ENGS:
            for o in self.ops[e]:
                for d in o.deps:
                    if d.kind == "c" and not (d.eng == "pe" and o.eng == "pe" and o.kind == "c"):
                        d.need = True
        self.sem = {}
        for e in self.ENGS:
            n = 0
            for o in self.ops[e]:
                if o.kind == "c" and o.need:
                    n += 1
                    o.sig = n
            self.sem[e] = es.enter_context(nc.semaphore("s_" + e))
        self.dsem = {}
        for i, key in enumerate(self.dcnt):
            self.dsem[key] = es.enter_context(nc.semaphore("d%d" % i))

    def emit(self, engname, e, final_keys=()):
        waited = {}
        for o in self.ops[engname]:
            ws = {}
            for d in o.deps:
                if d.kind == "c":
                    if d.eng == "pe" and engname == "pe" and o.kind == "c":
                        continue
                    sk = ("c", d.eng); val = d.sig
                else:
                    sk = ("d", d.key); val = d.cnt
                if ws.get(sk, 0) < val:
                    ws[sk] = val
            for sk, val in ws.items():
                if waited.get(sk, 0) >= val:
                    continue
                waited[sk] = val
                sem = self.sem[sk[1]] if sk[0] == "c" else self.dsem[sk[1]]
                e.wait_ge(sem, val)
            ins = o.fn(e)
            if o.kind == "c":
                if o.need:
                    ins.then_inc(self.sem[engname], 1)
            else:
                ins.then_inc(self.dsem[o.key], 16)
        for key in final_keys:
            if key in self.dcnt:
                e.wait_ge(self.dsem[key], self.dcnt[key])


def build_program(layers, from_x=True):
    nc = bass.Bass("TRN2", target_bir_lowering=False)
    P = Prog()

    def dram(name, shape, kind="ExternalInput"):
        return nc.dram_tensor(name, shape, F32, kind=kind).ap()

    x_d = dram("x", [S, D])
    out_d = dram("out", [S, D], kind="ExternalOutput")
    wA_d = dram("wA", [2, 4, 128, 8 * 512])
    wBk_d = dram("wBk", [2, 2, 128, 8 * 192])
    wBq_d = dram("wBq", [2, 4, 128, 8 * 256])
    wEo_d = dram("wEo", [2, 128, 8 * 1024])
    wG1_d = dram("wG1", [2, 128, 8 * 768])
    wG2_d = dram("wG2", [2, 128, 8 * 832])
    wQb_d = dram("wQb", [2, 16, 128, 2 * 192])
    wKvb_d = dram("wKvb", [2, 128, 2048])
    wOo_d = dram("wOo", [2, 128, 8 * 1024])
    colsE_d = dram("colsE", [2, 128, 8])
    lamE_d = dram("lamE", [2, 128, 256])
    lnE_d = dram("lnE", [2, 128, 2048])
    colsO_d = dram("colsO", [2, 128, 8])
    lnO_d = dram("lnO", [2, 128, 2048])
    rope_d = dram("rope", [6, 128, S])
    cst_d = dram("cst", [128, 256])
    perm_d = dram("perm", [128, 256])

    es = ExitStack()

    def sb(name, shape, dt):
        return es.enter_context(nc.sbuf_tensor(name, shape, dt))

    XT = sb("XT", [128, 8 * S], BF16)
    OG = sb("OG", [128, 8 * S], BF16)
    QK = sb("QK", [128, max(2 * S, 8192)], BF16)
    AR = sb("AR", [128, 7168], BF16)
    WBF = sb("WBF", [128, 8 * WCOLS], BF16)
    WST = sb("WST", [128, 2 * 832], F32)
    ROPE = sb("ROPE", [128, 1024], F32)
    TMP = sb("TMP", [128, 3072], F32)
    LNP = sb("LNP", [128, 2048], F32)
    CST = sb("CST", [128, 256], F32)
    ONESF = sb("ONESF", [128, 128], F32)
    ONESB = sb("ONESB", [128, 128], BF16)
    COLS = sb("COLS", [128, 8], F32)
    LAM = sb("LAM", [128, 256], F32)
    PERMF = sb("PERMF", [128, 256], F32)
    PERM = sb("PERM", [128, 256], BF16)
    SM = sb("SM", [128, 32], F32)
    PS = es.enter_context(nc.psum_tensor("PS", [128, 4096], F32))

    XT3 = XT[:, :].rearrange("p (c t) -> p c t", c=8)
    qT = QK[:, 0:S]
    kT = QK[:, S:2 * S]
    Vb = AR[:, 0:NKT * 128]
    V3 = Vb.rearrange("p (k d) -> p k d", k=NKT)
    pTb = AR[:, 4096:7168]
    ARF = AR[:, :].bitcast(F32)
    assert tuple(ARF.shape) == (128, 3584), ARF.shape
    ident = CST[:, 0:128]
    blk = CST[:, 128:256]

    def xt(c, a, b):
        return XT[:, c * S + a: c * S + b]

    def og(c, a, b):
        return OG[:, c * S + a: c * S + b]

    def wb(c, a, b):
        return WBF[:, c * WCOLS + a: c * WCOLS + b]

    def bank(b):
        return PS[:, b * 512:(b + 1) * 512]

    def T(i):
        return TMP[:, i * 512:(i + 1) * 512]

    def pT(s):
        return pTb[:, s * 1536:(s + 1) * 1536]

    _bk = [0]

    def nbank():
        b = _bk[0] % 8
        _bk[0] += 1
        return b

    def mm(out, lhsT, rhs, start, stop, r, w, tile_position=None):
        if tile_position is None:
            P.op("pe", lambda e: e.matmul(out, lhsT=lhsT, rhs=rhs, start=start, stop=stop), r=r, w=w)
        else:
            P.op("pe", lambda e: e.matmul(out, lhsT=lhsT, rhs=rhs, start=start, stop=stop,
                                          tile_position=tile_position), r=r, w=w)

    def tr(out, in_, r, w):
        P.op("pe", lambda e: e.transpose(out, in_, ident), r=list(r) + [("c", "cst")], w=w)

    def act(out, in_, func, r, w, scale=1.0, bias=0.0, accum_out=None):
        if accum_out is None:
            P.op("act", lambda e: e.activation(out=out, in_=in_, func=func, bias=bias, scale=scale), r=r, w=w)
        else:
            P.op("act", lambda e: e.activation(out=out, in_=in_, func=func, bias=bias, scale=scale,
                                               accum_out=accum_out), r=r, w=w)

    def tt(eng, out, in0, in1, op, r, w):
        P.op(eng, lambda e: e.tensor_tensor(out=out, in0=in0, in1=in1, op=op), r=r, w=w)

    def stt(out, in0, scalar, in1, op0, op1, r, w):
        P.op("dve", lambda e: e.scalar_tensor_tensor(out=out, in0=in0, scalar=scalar, in1=in1, op0=op0, op1=op1),
             r=r, w=w)

    def ts(eng, out, in0, s1, s2, op0, op1, r, w):
        if s2 is None:
            P.op(eng, lambda e: e.tensor_scalar(out=out, in0=in0, scalar1=s1, scalar2=None, op0=op0), r=r, w=w)
        else:
            P.op(eng, lambda e: e.tensor_scalar(out=out, in0=in0, scalar1=s1, scalar2=s2, op0=op0, op1=op1), r=r, w=w)

    def cp(eng, out, in_, r, w):
        if eng == "act":
            P.op("act", lambda e: e.copy(out=out, in_=in_), r=r, w=w)
        else:
            P.op(eng, lambda e: e.tensor_copy(out=out, in_=in_), r=r, w=w)

    def recip(out, in_, r, w):
        P.op("dve", lambda e: e.reciprocal(out=out, in_=in_), r=r, w=w)

    def memset(eng, ap, val, w):
        P.op(eng, lambda e: e.memset(ap, val), w=w)

    def dma(out, in_, key, r, w, eng="sp"):
        P.dma(eng, lambda e: e.dma_start(out=out, in_=in_), key=key, r=r, w=w)

    KT = lambda t: ("T", t)
    KPS = lambda b: ("ps", b)

    dma(CST[:, :], cst_d[:, :], key="cst", r=[], w=[("c", "cst")])
    memset("pool", ONESF[:, :], 1.0, w=[("c", "onesf")])
    dma(PERMF[:, :], perm_d[:, :], key="perm", r=[], w=[("c", "permf")])
    cp("pool", PERM[:, :], PERMF[:, :], r=[("c", "permf")], w=[("c", "perm")])
    memset("pool", ONESB[:, :], 1.0, w=[("c", "onesb")])

    _wst = [0]

    WK_ALL = ("wbf", "wbf1", "wbf2")

    def load_w(entries, keys=WK_ALL):
        for (src, c, c0) in entries:
            n = src.shape[-1]
            assert n <= 832
            s = _wst[0] % 2
            _wst[0] += 1
            st = WST[:, s * 832: s * 832 + n]
            dma(st, src, key=("wst", s), r=[], w=[("wst", s)])
            cp("pool", wb(c, c0, c0 + n), st, r=[("wst", s)], w=[(k_, c) for k_ in keys])

    def load_rope(tbl, t):
        dma(ROPE[:, 0:512], rope_d[2 * tbl, :, t * 512:(t + 1) * 512], key="ropeC", r=[], w=[("ropeC",)])
        dma(ROPE[:, 512:1024], rope_d[2 * tbl + 1, :, t * 512:(t + 1) * 512], key="ropeS", r=[], w=[("ropeS",)])

    ropeC = ROPE[:, 0:512]
    ropeS = ROPE[:, 512:1024]

    def proj_fm(bk, wcol0, M, t, nch=8, src=None, src_keys=None, wkey="wbf"):
        for c in range(nch):
            rhs = xt(c, t * 512, (t + 1) * 512) if src is None else src(c)
            mm(bank(bk)[0:M, :], wb(c, wcol0, wcol0 + M), rhs, c == 0, c == nch - 1,
               r=[(wkey, c), ("xT", t)], w=[KPS(bk)])

    def silu_to_og(bk, slab, t):
        act(og(slab, t * 512, (t + 1) * 512), bank(bk), AF.Silu, r=[KPS(bk)], w=[("og", slab, t)])

    def rstd_from_sq(src_ps_list, bss, lhs_const, lhs_key, inv_n, eps, out_t, nrows=128):
        for (b, ti) in src_ps_list:
            act(T(ti), bank(b), AF.Square, r=[KPS(b)], w=[KT(ti)])
        n = len(src_ps_list)
        for i, (b, ti) in enumerate(src_ps_list):
            mm(bank(bss), lhs_const, T(ti), i == 0, i == n - 1, r=[("c", lhs_key), KT(ti)], w=[KPS(bss)])
        act(T(out_t), bank(bss), AF.Ln, r=[KPS(bss)], w=[KT(out_t)], scale=inv_n, bias=eps)
        act(T(out_t), T(out_t), AF.Exp, r=[KT(out_t)], w=[KT(out_t)], scale=-0.5)

    def attention(kind, scale, slab, parity=0, lam_col=None, g_col=None, a_const=1.0):
        if kind == "A":
            groups = [[k] for k in range(NKT)]
            SG = [[0, 1], [2, 3]]
        else:
            groups = [list(range(g * 3, min(NKT, g * 3 + 3))) for g in range((NKT + 2) // 3)]
            SG = [[0, 1, 2], [3, 4, 5]]
        flat = [(q, gi, kts) for q in range(NTT) for gi, kts in enumerate(groups)]
        if kind == "A":
            rows = [(0, 64), (64, 128)]
        elif kind == "B":
            rows = [(parity * 64, parity * 64 + 64)]
        else:
            rows = [(0, 96)]

        sc_used = {}

        def scores(idx):
            q, gi, kts = flat[idx]
            sg = idx % 2
            used = []
            if kind == "A":
                k = kts[0]
                for c in range(2):
                    b = SG[sg][c]
                    a0, a1 = rows[c]
                    mm(bank(b), kT[a0:a1, k * 128:(k + 1) * 128], qT[a0:a1, q * 512:(q + 1) * 512], True, True,
                       r=[("kT", k // 4), ("qT", q)], w=[KPS(b)])
                    used.append(b)
            else:
                a0, a1 = rows[0]
                for i, k in enumerate(kts):
                    b = SG[sg][i]
                    rk = [("kT", k // 4), ("qT", q)]
                    if kind == "C":
                        rk.append(("kr", k // 4))
                        rk.append(("qTr", q))
                    mm(bank(b), kT[a0:a1, k * 128:(k + 1) * 128], qT[a0:a1, q * 512:(q + 1) * 512], True, True,
                       r=rk, w=[KPS(b)])
                    used.append(b)
            sc_used[idx] = used

        def scores_exp(idx):
            sg = idx % 2
            used = sc_used.pop(idx)
            n = len(used)
            b0 = used[0]
            act(pT(sg)[:, 0:n * 512], PS[:, b0 * 512:(b0 + n) * 512], AF.Exp,
                r=[KPS(b) for b in used], w=[("pT", sg)], scale=scale)

        def pv(idx):
            q, gi, kts = flat[idx]
            sg = idx % 2
            first = gi == 0
            last = gi == len(groups) - 1
            if kind == "A":
                k = kts[0]
                rr = [("V", k // 4), ("pT", sg)]
                mm(bank(4), V3[:, k, :], pT(sg)[:, 0:512], first, last, r=rr, w=[KPS(4)])
                mm(bank(6), V3[:, k, :], pT(sg)[:, 512:1024], first, last, r=rr, w=[KPS(6)])
                mm(bank(5)[0:64, :], ONESB[:, 0:64], pT(sg)[:, 0:512], first, last,
                   r=[("c", "onesb"), ("pT", sg)], w=[KPS(5)])
                mm(bank(5)[64:128, :], ONESB[:, 0:64], pT(sg)[:, 512:1024], first, last,
                   r=[("c", "onesb"), ("pT", sg)], w=[KPS(5)], tile_position=(0, 64))
            else:
                acc = 6 + (q % 2)
                for i, k in enumerate(kts):
                    mm(bank(acc), V3[:, k, :], pT(sg)[:, i * 512:(i + 1) * 512], first and i == 0,
                       last and i == len(kts) - 1, r=[("V", k // 4), ("Vones",), ("pT", sg)], w=[KPS(acc)])
            if last:
                epilogue(q)

        def epilogue(q):
            qs = (q * 512, (q + 1) * 512)
            if kind == "A":
                cp("dve", T(0), bank(4), r=[KPS(4)], w=[KT(0)])
                cp("dve", T(1), bank(6), r=[KPS(6)], w=[KT(1)])
                cp("dve", T(2), bank(5), r=[KPS(5)], w=[KT(2)])
                recip(T(2), T(2), r=[KT(2)], w=[KT(2)])
                lo_, hi_ = slice(0, 64), slice(64, 128)
                cp("dve", T(3)[lo_, :], T(2)[hi_, :], r=[KT(2)], w=[KT(3)])
                cp("dve", T(3)[hi_, :], T(2)[lo_, :], r=[KT(2)], w=[KT(3)])
                tt("dve", T(0)[lo_, :], T(0)[lo_, :], T(2)[lo_, :], ALU.mult, r=[KT(0), KT(2)], w=[KT(0)])
                tt("dve", T(0)[hi_, :], T(0)[hi_, :], T(3)[hi_, :], ALU.mult, r=[KT(0), KT(3)], w=[KT(0)])
                tt("dve", T(1)[lo_, :], T(1)[lo_, :], T(3)[lo_, :], ALU.mult, r=[KT(1), KT(3)], w=[KT(1)])
                tt("dve", T(1)[hi_, :], T(1)[hi_, :], T(2)[hi_, :], ALU.mult, r=[KT(1), KT(2)], w=[KT(1)])
                stt(T(0), T(1), lam_col, T(0), ALU.mult, ALU.add, r=[KT(0), KT(1), ("c", "lam")], w=[KT(0)])
                deferred.append([min(10, max(1, NKT // 2 - 1)), q, 0])
            else:
                acc = 6 + (q % 2)
                lo = slice(parity * 64, parity * 64 + 64)
                recip(T(2)[64:128, :], bank(acc)[64:128, :], r=[KPS(acc)], w=[KT(2)])
                tt("dve", T(3)[lo, :], bank(acc)[0:64, :], T(2)[64:128, :], ALU.mult, r=[KPS(acc), KT(2)], w=[KT(3)])
                o = og(slab, qs[0], qs[1])[lo, :]
                tt("dve", o, T(3)[lo, :], o, ALU.mult, r=[KT(3), ("og", slab, q)], w=[("og", slab, q)])

        deferred = []

        def epi_stage(q, stage, b):
            qs = (q * 512, (q + 1) * 512)
            if stage == 0:
                act(T(1), T(0), AF.Square, r=[KT(0)], w=[KT(1)])
            elif stage == 1:
                b = 7
                mm(bank(b), ONESF[:, :], T(1), True, True, r=[("c", "onesf"), KT(1)], w=[KPS(b)])
                act(T(1), bank(b), AF.Ln, r=[KPS(b)], w=[KT(1)], scale=1.0 / 128.0, bias=RMS_EPS)
                act(T(1), T(1), AF.Exp, r=[KT(1)], w=[KT(1)], scale=-0.5)
            else:
                stt(T(0), T(0), g_col, T(1), ALU.mult, ALU.mult, r=[KT(0), KT(1), ("c", "cols")], w=[KT(0)])
                o = og(slab, qs[0], qs[1])
                stt(o, T(0), a_const, o, ALU.mult, ALU.mult, r=[KT(0), ("og", slab, q)], w=[("og", slab, q)])

        epi_bank = {}

        def run_deferred(idx, flush=False):
            while deferred:
                d = deferred[0]
                d[0] -= 1
                if d[0] > 0 and not flush:
                    break
                epi_stage(d[1], d[2], SG[(idx + 1) % 2][0])
                d[2] += 1
                if d[2] > 2:
                    deferred.pop(0)
                else:
                    d[0] = 2 if NKT >= 32 else 1
                if not flush:
                    break

        scores(0)
        scores_exp(0)
        if len(flat) > 1:
            scores(1)
            scores_exp(1)
        for idx in range(len(flat)):
            run_deferred(idx)
            if idx + 2 < len(flat):
                scores(idx + 2)
            pv(idx)
            if idx + 2 < len(flat):
                scores_exp(idx + 2)
        def finish():
            while deferred:
                run_deferred(0, flush=True)

        if kind == "A":
            return finish
        finish()
        return None

    def phase_T():
        for i in range(NKT):
            if i % 2 == 0:
                buf = TMP[:, 0:1024]; bkeys = [KT(0), KT(1)]
            else:
                buf = ARF[:, 0:1024]; bkeys = [("arf", 0)]
            dma(buf, x_d[i * 128:(i + 1) * 128, :], key=("xin", i % 2), r=[], w=bkeys)
            b0 = nbank(); b1 = nbank()
            for c in range(8):
                b = b0 if c < 4 else b1
                tr(bank(b)[:, (c % 4) * 128:(c % 4 + 1) * 128], buf[:, c * 128:(c + 1) * 128], r=bkeys, w=[KPS(b)])
            for hb, b in ((0, b0), (1, b1)):
                cp("dve", XT3[:, hb * 4:(hb + 1) * 4, i * 128:(i + 1) * 128],
                   bank(b).rearrange("p (c t) -> p c t", c=4), r=[KPS(b)], w=[("xT", i // 4)])

    WOUT_PREFETCH = (3 * S + 8192) <= 8 * S

    def prefetch_wout(L):
        if not WOUT_PREFETCH:
            return
        wout_d = wEo_d[L // 2] if L % 2 == 0 else wOo_d[L // 2]
        xk = [("xT", t) for t in range(NTT)]
        for c in range(8):
            for h in range(2):
                s = _wst[0] % 2
                _wst[0] += 1
                st = WST[:, s * 832: s * 832 + 512]
                dma(st, wout_d[:, c * 1024 + h * 512: c * 1024 + (h + 1) * 512], key=("wst", s), r=[],
                    w=[("wst", s)])
                o0 = 3 * S + c * 1024 + h * 512
                cp("pool", XT[:, o0:o0 + 512], st, r=[("wst", s)], w=xk)

    def phase_E(L, wout_d, ln_d, last, first):
        src_d = x_d if first else out_d
        qk_keys_lo = [("qT", t) for t in range(NTT)] + [("qTr", t) for t in range(NTT)]
        qk_keys_hi = [("kT", t) for t in range(NTT)] + [("kr", t) for t in range(NTT)]
        xt_keys = [("xT", t) for t in range(NTT)]
        if WOUT_PREFETCH:
            for q4 in range(4):
                eng = "dve" if q4 % 2 == 0 else "act"
                cp(eng, QK[:, q4 * 2048:(q4 + 1) * 2048], XT[:, 3 * S + q4 * 2048: 3 * S + (q4 + 1) * 2048],
                   r=xt_keys, w=(qk_keys_lo if q4 < 2 else qk_keys_hi))
        else:
            for c in range(8):
                for h in range(2):
                    s = _wst[0] % 2
                    _wst[0] += 1
                    st = WST[:, s * 832: s * 832 + 512]
                    dma(st, wout_d[:, c * 1024 + h * 512: c * 1024 + (h + 1) * 512], key=("wst", s), r=[],
                        w=[("wst", s)])
                    cp("pool", QK[:, c * 1024 + h * 512: c * 1024 + (h + 1) * 512], st, r=[("wst", s)],
                       w=(qk_keys_lo if c < 4 else qk_keys_hi))
        dma(LNP[:, :], ln_d[:, :], key="lnp", r=[], w=[("c", "lnp")])
        lng = LNP[:, 0:1024]
        lnb = LNP[:, 1024:2048]
        junk = ROPE[:, 0:1024]
        sets = [
            (TMP[:, 0:1024], TMP[:, 1024:2048], [KT(0), KT(1)], [KT(2), KT(3)]),
            (ARF[:, 0:1024], ARF[:, 1024:2048], [("arf", 0)], [("arf", 1)]),
            (TMP[:, 2048:3072], ARF[:, 2048:3072], [KT(4), KT(5)], [("arf", 2)]),
        ]
        def ctx(i):
            si = i % 3
            xr, xn, kx, kn = sets[si]
            return slice(i * 128, (i + 1) * 128), xr, xn, kx, kn, SM[:, si * 8: si * 8 + 8], ("sm", si), si

        def ybanks(i):
            return 2 * (i % 3), 2 * (i % 3) + 1

        def e_y(i):
            b0, b1 = ybanks(i)
            for h, b in ((0, b0), (1, b1)):
                for c in range(8):
                    mm(bank(b), og(c, i * 128, (i + 1) * 128), QK[:, c * 1024 + h * 512: c * 1024 + (h + 1) * 512],
                       c == 0, c == 7, r=[("og", c, i // 4)] + (qk_keys_lo if c < 4 else qk_keys_hi), w=[KPS(b)])

        def e_ld(i):
            ts_, xr, xn, kx, kn, sm, ksm, si = ctx(i)
            dma(xr, src_d[ts_, :], key=("xin", si), r=[("xhbm", i)], w=kx)

        def e_r(i):
            ts_, xr, xn, kx, kn, sm, ksm, si = ctx(i)
            b0, b1 = ybanks(i)
            for h, b in ((0, b0), (1, b1)):
                stt(xr[:, h * 512:(h + 1) * 512], xr[:, h * 512:(h + 1) * 512], float(ALPHA), bank(b),
                    ALU.mult, ALU.add, r=kx + [KPS(b)], w=kx)

        def e_stats(i):
            ts_, xr, xn, kx, kn, sm, ksm, si = ctx(i)
            act(junk, xr, AF.Identity, r=kx, w=[ksm, ("junk",)], accum_out=sm[:, 0:1])
            act(junk, xr, AF.Square, r=kx, w=[ksm, ("junk",)], accum_out=sm[:, 1:2])

        def e_smalls(i):
            ts_, xr, xn, kx, kn, sm, ksm, si = ctx(i)
            ts("dve", sm[:, 2:3], sm[:, 0:1], -1.0 / D, None, ALU.mult, None, r=[ksm], w=[ksm])
            tt("dve", sm[:, 3:4], sm[:, 2:3], sm[:, 2:3], ALU.mult, r=[ksm], w=[ksm])
            stt(sm[:, 4:5], sm[:, 1:2], 1.0 / D, sm[:, 3:4], ALU.mult, ALU.subtract, r=[ksm], w=[ksm])

        def e_lnexp(i):
            ts_, xr, xn, kx, kn, sm, ksm, si = ctx(i)
            act(sm[:, 5:6], sm[:, 4:5], AF.Ln, r=[ksm], w=[ksm], bias=LN_EPS)
            act(sm[:, 5:6], sm[:, 5:6], AF.Exp, r=[ksm], w=[ksm], scale=-0.5)
            act(sm[:, 6:7], sm[:, 2:3], AF.Identity, r=[ksm], w=[ksm], scale=sm[:, 5:6])

        def e_nmr(i):
            return

        def e_xn(i):
            ts_, xr, xn, kx, kn, sm, ksm, si = ctx(i)
            act(xn, xr, AF.Identity, r=kx + [ksm], w=kn, scale=sm[:, 5:6], bias=sm[:, 6:7])

        def e_gb(i):
            ts_, xr, xn, kx, kn, sm, ksm, si = ctx(i)
            tt("dve", xn, xn, lng, ALU.mult, r=kn + [("c", "lnp")], w=kn)
            tt("dve", xn, xn, lnb, ALU.add, r=kn + [("c", "lnp")], w=kn)
            dma(out_d[ts_, :], xn, key=("xout", si), r=kn, w=[("xhbm", i)], eng="pool")

        def e_tr(i):
            if last:
                return
            ts_, xr, xn, kx, kn, sm, ksm, si = ctx(i)
            for c in range(8):
                b = 6 if c < 4 else 7
                tr(bank(b)[:, (c % 4) * 128:(c % 4 + 1) * 128], xn[:, c * 128:(c + 1) * 128], r=kn, w=[KPS(b)])

        def e_casts(i):
            if last:
                return
            for hb, b in ((0, 6), (1, 7)):
                cp("act", XT3[:, hb * 4:(hb + 1) * 4, i * 128:(i + 1) * 128],
                   bank(b).rearrange("p (c t) -> p c t", c=4), r=[KPS(b)], w=[("xT", i // 4)])

        plan = [(e_y, 0), (e_smalls, 3), (e_lnexp, 3), (e_gb, 4), (e_tr, 4), (e_nmr, 3), (e_xn, 3),
                (e_r, 2), (e_stats, 2), (e_casts, 4), (e_ld, 0)]
        for k in range(NKT + 4):
            for fn, off in plan:
                i = k - off
                if 0 <= i < NKT:
                    fn(i)

    def w_entries(d_ap, ncols, nch=8, c_base=0, col0=0):
        return [(d_ap[:, c * ncols:(c + 1) * ncols], c_base + c, col0) for c in range(nch)]

    def even_first_weights(i):
        load_w(w_entries(wA_d[i, 0], 512))

    def rope_combine(bq, bqs, out_ap, out_keys, rows=slice(0, 128), tp=(0, 1)):
        ta, tb = tp
        tt("dve", T(ta)[rows, :], bank(bq)[rows, :], ropeC[rows, :], ALU.mult, r=[KPS(bq), ("ropeC",)], w=[KT(ta)])
        tt("dve", T(tb)[rows, :], bank(bqs)[rows, :], ropeS[rows, :], ALU.mult, r=[KPS(bqs), ("ropeS",)], w=[KT(tb)])
        tt("pool", out_ap, T(ta)[rows, :], T(tb)[rows, :], ALU.add, r=[KT(ta), KT(tb)], w=out_keys)

    def rms_rope(bq, bqs, gcol, gswcol, out_ap, out_keys):
        bss = nbank()
        rstd_from_sq([(bq, 2)], bss, blk, "cst", 1.0 / 64.0, RMS_EPS, 3)
        stt(T(0), bank(bq), gcol, ropeC, ALU.mult, ALU.mult, r=[KPS(bq), ("ropeC",), ("c", "cols")], w=[KT(0)])
        stt(T(1), bank(bqs), gswcol, ropeS, ALU.mult, ALU.mult, r=[KPS(bqs), ("ropeS",), ("c", "cols")], w=[KT(1)])
        tt("pool", T(0), T(0), T(1), ALU.add, r=[KT(0), KT(1)], w=[KT(0)])
        tt("pool", out_ap, T(0), T(3), ALU.mult, r=[KT(0), KT(3)], w=out_keys)

    def v_proj(t, wcol0, dv, nch=8, src=None):
        bv = nbank()
        for j in range(4):
            tok = (t * 4 + j) * 128
            for c in range(nch):
                lhs = xt(c, tok, tok + 128) if src is None else src(c, tok)
                mm(bank(bv)[:, j * dv:(j + 1) * dv], lhs, wb(c, wcol0, wcol0 + dv), c == 0, c == nch - 1,
                   r=[("wbf", c), ("xT", t)], w=[KPS(bv)])
        cp("act", V3[:, t * 4:(t + 1) * 4, 0:dv], bank(bv)[:, 0:4 * dv].rearrange("p (k d) -> p k d", k=4),
           r=[KPS(bv)], w=[("V", t)])

    def QB(slot, sub=0):
        return pT(slot)[:, sub * 512:(sub + 1) * 512]

    def a_stage1(t, col0, slot):
        b = nbank()
        proj_fm(b, col0, 128, t)
        cp("act", QB(slot, t % 2), bank(b), r=[KPS(b)], w=[("pT", slot)])
        return b

    def a_stage2(t, b, slot, out_ap, out_keys, tp):
        bs = nbank()
        mm(bank(bs), PERM[:, 0:128], QB(slot, t % 2), True, True, r=[("c", "perm"), ("pT", slot)], w=[KPS(bs)])
        rope_combine(b, bs, out_ap, out_keys, tp=tp)

    def b_stage1(t, col0, slot, gcol, wkey="wbf"):
        b = nbank()
        proj_fm(b, col0, 128, t, wkey=wkey)
        act(QB(slot), bank(b), AF.Copy, r=[KPS(b), ("c", "cols")], w=[("pT", slot)], scale=gcol)
        act(T(2 + 2 * slot), bank(b), AF.Square, r=[KPS(b)], w=[KT(2 + 2 * slot)])
        return b

    def b_stage2(t, b, slot, gcol, out_ap, out_keys):
        bs = nbank(); bss = nbank()
        tsq = 2 + 2 * slot
        trs = 3 + 2 * slot
        mm(bank(bs), PERM[:, 128:256], QB(slot), True, True, r=[("c", "perm"), ("pT", slot)], w=[KPS(bs)])
        mm(bank(bss), blk, T(tsq), True, True, r=[("c", "cst"), KT(tsq)], w=[KPS(bss)])
        act(T(trs), bank(bss), AF.Ln, r=[KPS(bss)], w=[KT(trs)], scale=1.0 / 64.0, bias=RMS_EPS)
        act(T(trs), T(trs), AF.Exp, r=[KT(trs)], w=[KT(trs)], scale=-0.5)
        stt(T(0), bank(b), gcol, ropeC, ALU.mult, ALU.mult, r=[KPS(b), ("ropeC",), ("c", "cols")], w=[KT(0)])
        tt("dve", T(1), bank(bs), ropeS, ALU.mult, r=[KPS(bs), ("ropeS",)], w=[KT(1)])
        tt("pool", T(0), T(0), T(1), ALU.add, r=[KT(0), KT(1)], w=[KT(0)])
        tt("pool", out_ap, T(0), T(trs), ALU.mult, r=[KT(0), KT(trs)], w=out_keys)

    def even_layer(L, nxt_loader):
        i = L // 2
        lam_init = 0.8 - 0.6 * math.exp(-0.3 * L)
        dma(COLS[:, :], colsE_d[i], key="cols", r=[], w=[("c", "cols")])
        dma(LAM[:, :], lamE_d[i], key="lam", r=[], w=[("c", "lamraw")])
        tt("dve", LAM[:, 0:64], LAM[:, 0:64], LAM[:, 64:128], ALU.mult, r=[("c", "lamraw")], w=[("c", "lamraw")])
        tt("dve", LAM[:, 128:192], LAM[:, 128:192], LAM[:, 192:256], ALU.mult, r=[("c", "lamraw")], w=[("c", "lamraw")])
        P.op("dve", lambda e: e.reduce_sum(out=SM[:, 16:17], in_=LAM[:, 0:64], axis=mybir.AxisListType.X),
             r=[("c", "lamraw")], w=[("sm", 2)])
        P.op("dve", lambda e: e.reduce_sum(out=SM[:, 17:18], in_=LAM[:, 128:192], axis=mybir.AxisListType.X),
             r=[("c", "lamraw")], w=[("sm", 2)])
        act(SM[:, 16:18], SM[:, 16:18], AF.Exp, r=[("sm", 2)], w=[("sm", 2)])
        tt("dve", SM[:, 18:19], SM[:, 17:18], SM[:, 16:17], ALU.subtract, r=[("sm", 2)], w=[("sm", 2)])
        ts("dve", SM[:, 19:20], SM[:, 18:19], -lam_init, None, ALU.add, None, r=[("sm", 2)], w=[("c", "lam")])
        neglam = SM[:, 19:20]

        pend = None
        for h in range(4):
            for t in range(NTT):
                bg = nbank()
                proj_fm(bg, 384, 128, t)
                silu_to_og(bg, h, t)
            if pend is not None:
                pend()
                pend = None
            prev = None
            for t in range(NTT + 1):
                cur = None
                if t < NTT:
                    cur = (a_stage1(t, 0, 0), a_stage1(t, 128, 1))
                if prev is not None:
                    tp_ = t - 1
                    load_rope(0, tp_)
                    a_stage2(tp_, prev[0], 0, qT[:, tp_ * 512:(tp_ + 1) * 512], [("qT", tp_), ("qTr", tp_)], (0, 1))
                    a_stage2(tp_, prev[1], 1, kT[:, tp_ * 512:(tp_ + 1) * 512], [("kT", tp_), ("kr", tp_)], (4, 5))
                    v_proj(tp_, 256, 128)
                prev = cur
            if h < 3:
                load_w(w_entries(wA_d[i, h + 1], 512))
            else:
                load_w(w_entries(wBq_d[i, 0], 256, col0=192), keys=("wbf", "wbf1", "wbf2"))
                load_w(w_entries(wBq_d[i, 1], 256, col0=448), keys=("wbf", "wbf2"))
                load_w(w_entries(wBk_d[i, 0], 192), keys=("wbf",))
            pend = attention("A", 0.125, h, lam_col=neglam, g_col=COLS[:, 0:1], a_const=float(1.0 - lam_init))

        memset("pool", V3[:, :, 64:128], 1.0, w=[("Vones",)] + [("V", t) for t in range(NTT)])
        for g in range(2):
            prev = None
            for t in range(NTT + 1):
                cur = None
                if t < NTT:
                    cur = b_stage1(t, 0, t % 2, COLS[:, 3:4])
                if pend is not None and (t == 1 or NTT == 1):
                    pend()
                    pend = None
                if prev is not None:
                    tp_ = t - 1
                    load_rope(1, tp_)
                    b_stage2(tp_, prev, tp_ % 2, COLS[:, 3:4], kT[:, tp_ * 512:(tp_ + 1) * 512],
                             [("kT", tp_), ("kr", tp_)])
                    v_proj(tp_, 128, 64)
                prev = cur
            for pp in range(2):
                slab = 4 + g * 2 + pp
                wk_ = "wbf1" if pp == 0 else "wbf2"
                cb_ = 192 + pp * 256
                for t in range(NTT):
                    bg = nbank()
                    proj_fm(bg, cb_ + 128, 128, t, wkey=wk_)
                    silu_to_og(bg, slab, t)
                prev = None
                for t in range(NTT + 1):
                    cur = None
                    if t < NTT:
                        cur = b_stage1(t, cb_, t % 2, COLS[:, 1:2], wkey=wk_)
                    if prev is not None:
                        tp_ = t - 1
                        load_rope(1, tp_)
                        b_stage2(tp_, prev, tp_ % 2, COLS[:, 1:2], qT[:, tp_ * 512:(tp_ + 1) * 512],
                                 [("qT", tp_), ("qTr", tp_)])
                    prev = cur
                if pp == 0:
                    pass
                elif g == 0:
                    load_w(w_entries(wBq_d[i, 2], 256, col0=192), keys=("wbf1",))
                    load_w(w_entries(wBq_d[i, 3], 256, col0=448), keys=("wbf2",))
                    load_w(w_entries(wBk_d[i, 1], 192), keys=("wbf",))
                else:
                    nxt_loader()
                    prefetch_wout(L)
                for jj in range(2):
                    attention("B", 0.125, slab, parity=jj)

    def odd_first_weights(i):
        load_w([(wG1_d[i][:, c * 768: c * 768 + 384], c, 0) for c in range(8)])
        load_w([(wG1_d[i][:, c * 768 + 384: c * 768 + 768], c, 448) for c in range(8)])

    def ropeCp(t):
        return LNP[(t % 4) * 32:(t % 4) * 32 + 32, (t // 4) * 512:(t // 4 + 1) * 512]

    def ropeSp(t):
        return LNP[(t % 4) * 32:(t % 4) * 32 + 32, 1024 + (t // 4) * 512:1024 + (t // 4 + 1) * 512]

    def load_ropep():
        for t in range(NTT):
            dma(ropeCp(t), rope_d[4, 64:96, t * 512:(t + 1) * 512], key="ropep", r=[], w=[("c", "ropep")])
            dma(ropeSp(t), rope_d[5, 64:96, t * 512:(t + 1) * 512], key="ropep", r=[], w=[("c", "ropep")])

    def odd_layer(L, nxt_loader):
        i = L // 2
        load_ropep()
        dma(COLS[:, :], colsO_d[i], key="cols", r=[], w=[("c", "cols")])
        for t in range(NTT):
            for s_ in range(3):
                bg = nbank()
                proj_fm(bg, s_ * 128, 128, t, wkey="wbf")
                silu_to_og(bg, s_, t)
        load_w([(wG2_d[i][:, c * 832: c * 832 + 256], c, 0) for c in range(8)], keys=("wbf",))
        load_w([(wG2_d[i][:, c * 832 + 640: c * 832 + 832], c, 256) for c in range(8)], keys=("wbf",))
        memset("pool", V3[:, :, 64:128], 1.0, w=[("Vones",)] + [("V", t) for t in range(NTT)])
        for t in range(NTT):
            for s_ in range(3):
                bg = nbank()
                proj_fm(bg, 448 + s_ * 128, 128, t, wkey="wbf2")
                silu_to_og(bg, 3 + s_, t)
        load_w([(wG2_d[i][:, c * 832 + 256: c * 832 + 640], c, 448) for c in range(8)], keys=("wbf2",))
        for t in range(NTT):
            for s_ in range(2):
                bg = nbank()
                proj_fm(bg, s_ * 128, 128, t, wkey="wbf")
                silu_to_og(bg, 6 + s_, t)
        for t in range(NTT):
            bkr = nbank(); bkrs = nbank()
            proj_fm(bkr, 256, 96, t, wkey="wbf"); proj_fm(bkrs, 352, 96, t, wkey="wbf")
            r64 = slice(64, 96)
            tt("dve", T(0)[r64, :], bank(bkr)[r64, :], ropeCp(t), ALU.mult, r=[KPS(bkr), ("c", "ropep")], w=[KT(0)])
            tt("dve", T(1)[r64, :], bank(bkrs)[r64, :], ropeSp(t), ALU.mult, r=[KPS(bkrs), ("c", "ropep")], w=[KT(1)])
            tt("pool", kT[64:96, t * 512:(t + 1) * 512], T(0)[r64, :], T(1)[r64, :], ALU.add, r=[KT(0), KT(1)],
               w=[("kr", t)])
        for t in range(NTT):
            bc0 = nbank(); bc1 = nbank(); bkv = nbank()
            proj_fm(bc0, 448, 128, t, wkey="wbf2"); proj_fm(bc1, 576, 128, t, wkey="wbf2")
            proj_fm(bkv, 704, 128, t, wkey="wbf2")
            bss = nbank()
            rstd_from_sq([(bc0, 2), (bc1, 3)], bss, ONESF[:, :], "onesf", 1.0 / 256.0, RMS_EPS, 5)
            stt(xt(0, t * 512, (t + 1) * 512), bank(bc0), COLS[:, 0:1], T(5), ALU.mult, ALU.mult,
                r=[KPS(bc0), KT(5), ("c", "cols")], w=[("xT", t)])
            stt(xt(1, t * 512, (t + 1) * 512), bank(bc1), COLS[:, 1:2], T(5), ALU.mult, ALU.mult,
                r=[KPS(bc1), KT(5), ("c", "cols")], w=[("xT", t)])
            bss2 = nbank()
            rstd_from_sq([(bkv, 2)], bss2, ONESF[:, :], "onesf", 1.0 / 128.0, RMS_EPS, 3)
            stt(xt(2, t * 512, (t + 1) * 512), bank(bkv), COLS[:, 2:3], T(3), ALU.mult, ALU.mult,
                r=[KPS(bkv), KT(3), ("c", "cols")], w=[("xT", t)])

        def c_weights(j, keys=WK_ALL):
            ent = [(wQb_d[i, j][:, c * 192:(c + 1) * 192], c, 0) for c in range(2)]
            ent.append((wKvb_d[i][:, j * 128:(j + 1) * 128], 2, 0))
            load_w(ent, keys=keys)

        c_weights(0, keys=("wbf",))
        sc = 96.0 ** -0.5
        for j in range(16):
            for t in range(NTT):
                bq = nbank(); bqs = nbank()
                proj_fm(bq, 0, 96, t, nch=2); proj_fm(bqs, 96, 96, t, nch=2)
                r64 = slice(64, 96)
                tsl = slice(t * 512, (t + 1) * 512)
                ta, tb = (0, 1) if t % 2 == 0 else (4, 5)
                tt("dve", T(ta)[r64, :], bank(bq)[r64, :], ropeCp(t), ALU.mult, r=[KPS(bq), ("c", "ropep")], w=[KT(ta)])
                tt("dve", T(tb)[r64, :], bank(bqs)[r64, :], ropeSp(t), ALU.mult, r=[KPS(bqs), ("c", "ropep")], w=[KT(tb)])
                cp("act", qT[0:64, tsl], bank(bq)[0:64, :], r=[KPS(bq)], w=[("qT", t)])
                tt("pool", qT[64:96, tsl], T(ta)[r64, :], T(tb)[r64, :], ALU.add, r=[KT(ta), KT(tb)], w=[("qTr", t)])
                bk = nbank()
                mm(bank(bk)[0:64, :], wb(2, 0, 64), xt(2, t * 512, (t + 1) * 512), True, True,
                   r=[("wbf", 2), ("xT", t)], w=[KPS(bk)])
                cp("act", kT[0:64, t * 512:(t + 1) * 512], bank(bk)[0:64, :], r=[KPS(bk)], w=[("kT", t)])
                bv = nbank()
                for jj in range(4):
                    tok = (t * 4 + jj) * 128
                    mm(bank(bv)[:, jj * 64:(jj + 1) * 64], xt(2, tok, tok + 128), wb(2, 64, 128), True, True,
                       r=[("wbf", 2), ("xT", t)], w=[KPS(bv)])
                cp("act", V3[:, t * 4:(t + 1) * 4, 0:64], bank(bv)[:, 0:256].rearrange("p (k d) -> p k d", k=4),
                   r=[KPS(bv)], w=[("V", t)])
            if j < 15:
                c_weights(j + 1)
            else:
                nxt_loader()
                prefetch_wout(L)
            attention("C", sc, j // 2, parity=j % 2)

    def first_weights(L):
        if L % 2 == 0:
            even_first_weights(L // 2)
        else:
            odd_first_weights(L // 2)

    if from_x and layers[0] == 0:
        phase_T()
    else:
        phase_T()
    first_weights(layers[0])
    P.barrier()
    for li, L in enumerate(layers):
        last = li == len(layers) - 1
        nxt = (lambda L2=layers[li + 1]: first_weights(L2)) if not last else (lambda: None)
        if L % 2 == 0:
            even_layer(L, nxt)
        else:
            odd_layer(L, nxt)
        P.barrier()
        if L % 2 == 0:
            phase_E(L, wEo_d[L // 2], lnE_d[L // 2], last, li == 0)
        else:
            phase_E(L, wOo_d[L // 2], lnO_d[L // 2], last, li == 0)
        P.barrier()

    P.finalize(nc, es)
    with nc.Block() as block:
        @block.tensor
        def _(e):
            P.emit("pe", e)

        @block.scalar
        def _(e):
            P.emit("act", e)

        @block.vector
        def _(e):
            P.emit("dve", e)

        @block.gpsimd
        def _(e):
            P.emit("pool", e)

        @block.sync
        def _(e):
            P.emit("sp", e, final_keys=[("xout", 0), ("xout", 1), ("xout", 2)])
    es.close()
    return nc, P


def _tile_w(w, ncols):
    k = w.shape[0]
    return np.ascontiguousarray(w.reshape(k // 128, 128, ncols).transpose(1, 0, 2).reshape(128, -1))


def _gather_cols(w, idx):
    idx = np.asarray(idx)
    o = np.zeros((w.shape[0], len(idx)), np.float32)
    m = idx >= 0
    o[:, m] = w[:, idx[m]]
    return o


def _partner(d, half):
    return d + half if (d % (2 * half)) < half else d - half


def _rope_tables():
    pos = np.arange(S, dtype=np.float32)
    row = (np.arange(S) // 64).astype(np.float32)
    col = (np.arange(S) % 64).astype(np.float32)

    def angles(p, dims, theta):
        inv = np.float32(theta) ** (-(np.arange(0, dims, 2, dtype=np.float32) / np.float32(dims)))
        inv = inv.astype(np.float32)
        ang = (p[None, :] * inv[:, None]).astype(np.float32)
        return np.cos(ang).astype(np.float32), np.sin(ang).astype(np.float32)

    tabs = np.zeros((6, 128, S), np.float32)
    tabs[0::2] = 1.0
    ca, sa = angles(pos, 16, 500000.0)
    for p in range(128):
        d = p % 64
        if d < 16:
            f = d % 8
            tabs[0, p] = ca[f]
            tabs[1, p] = -sa[f] if d < 8 else sa[f]
    cr, sr = angles(row, 32, 10000.0)
    cc, sc = angles(col, 32, 10000.0)
    for p in range(128):
        d = p % 64
        if d < 32:
            f = d % 16
            tabs[2, p] = cr[f]
            tabs[3, p] = -sr[f] if d < 16 else sr[f]
        else:
            dd = d - 32
            f = dd % 16
            tabs[2, p] = cc[f]
            tabs[3, p] = -sc[f] if dd < 16 else sc[f]
    c3, s3 = angles(pos, 32, 500000.0)
    for p in range(64, 96):
        dd = p - 64
        f = dd % 16
        tabs[4, p] = c3[f]
        tabs[5, p] = -s3[f] if dd < 16 else s3[f]
    return tabs


def _prep_weights(inp):
    f = lambda a: np.asarray(a, dtype=np.float32)
    ev_w_in, ev_w_out = f(inp["ev_w_in"]), f(inp["ev_w_out"])
    od_w_in, od_w_out = f(inp["od_w_in"]), f(inp["od_w_out"])
    od_w_qb, od_w_kvb = f(inp["od_w_qb"]), f(inp["od_w_kvb"])
    wA = np.zeros((2, 4, 128, 8 * 512), np.float32)
    wBk = np.zeros((2, 2, 128, 8 * 192), np.float32)
    wBq = np.zeros((2, 4, 128, 8 * 256), np.float32)
    wEo = np.zeros((2, 128, 8 * 1024), np.float32)
    colsE = np.zeros((2, 128, 8), np.float32)
    lamE = np.zeros((2, 128, 256), np.float32)
    lnE = np.zeros((2, 128, 2048), np.float32)
    pa = np.array([(_partner(d, 8) if d < 16 else -1) for d in range(64)])
    pb = np.array([_partner(d, 16) for d in range(64)])
    for i in range(2):
        W = ev_w_in[i]
        for h in range(4):
            q = np.arange(128) + h * 128
            qsw = np.array([(h * 128 + (p // 64) * 64 + pa[p % 64]) if pa[p % 64] >= 0 else -1 for p in range(128)])
            k = q + 512
            ksw = np.where(qsw >= 0, qsw + 512, -1)
            v = np.arange(128) + 1024 + h * 128
            g = np.arange(128) + 2304 + h * 128
            idx = np.concatenate([q, k, v, g])
            wA[i, h] = _tile_w(_gather_cols(W, idx), 512)
        for g_ in range(2):
            k = 2048 + g_ * 64 + np.arange(64)
            ksw = 2048 + g_ * 64 + pb
            v = 2176 + g_ * 64 + np.arange(64)
            idx = np.concatenate([k, k, v])
            wBk[i, g_] = _tile_w(_gather_cols(W, idx), 192)
            for pp in range(2):
                j0 = g_ * 4 + pp * 2
                q = 1536 + j0 * 64 + np.arange(128)
                qsw = np.array([1536 + (j0 + p // 64) * 64 + pb[p % 64] for p in range(128)])
                gt = 2304 + 512 + j0 * 64 + np.arange(128)
                idx = np.concatenate([q, gt])
                wBq[i, g_ * 2 + pp] = _tile_w(_gather_cols(W, idx), 256)
        wEo[i] = _tile_w(ev_w_out[i], 1024)
        qn, kn = f(inp["ev_qnorm"])[i], f(inp["ev_knorm"])[i]
        colsE[i, :, 0] = f(inp["ev_subln"])[i]
        colsE[i, :, 1] = np.tile(qn, 2)
        colsE[i, :, 2] = np.tile(qn[pb], 2)
        colsE[i, :, 3] = np.tile(kn, 2)
        colsE[i, :, 4] = np.tile(kn[pb], 2)
        lamE[i] = np.broadcast_to(f(inp["ev_lam"])[i].reshape(1, 256), (128, 256))
        lnE[i, :, 0:1024] = np.broadcast_to(f(inp["ev_ln_g"])[i][None, :], (128, 1024))
        lnE[i, :, 1024:2048] = np.broadcast_to(f(inp["ev_ln_b"])[i][None, :], (128, 1024))
    wG1 = np.zeros((2, 128, 8 * 768), np.float32)
    wG2 = np.zeros((2, 128, 8 * 832), np.float32)
    wQb = np.zeros((2, 16, 128, 2 * 192), np.float32)
    wKvb = np.zeros((2, 128, 2048), np.float32)
    wOo = np.zeros((2, 128, 8 * 1024), np.float32)
    colsO = np.zeros((2, 128, 8), np.float32)
    lnO = np.zeros((2, 128, 2048), np.float32)
    pc = np.array([_partner(d, 16) for d in range(32)])
    for i in range(2):
        W = od_w_in[i]
        wG1[i] = _tile_w(_gather_cols(W, 416 + np.arange(768)), 768)
        kr = np.concatenate([-np.ones(64, int), 384 + np.arange(32)])
        krsw = np.concatenate([-np.ones(64, int), 384 + pc])
        idx = np.concatenate([416 + 768 + np.arange(256), np.arange(256), 256 + np.arange(128), kr, krsw])
        wG2[i] = _tile_w(_gather_cols(W, idx), 832)
        for j in range(16):
            q = j * 96 + np.arange(96)
            qsw = np.concatenate([-np.ones(64, int), j * 96 + 64 + pc])
            wQb[i, j] = _tile_w(_gather_cols(od_w_qb[i], np.concatenate([q, qsw])), 192)
        wKvb[i] = od_w_kvb[i]
        wOo[i] = _tile_w(od_w_out[i], 1024)
        qn = f(inp["od_qnorm"])[i]
        colsO[i, :, 0] = qn[0:128]
        colsO[i, :, 1] = qn[128:256]
        colsO[i, :, 2] = f(inp["od_kvnorm"])[i]
        lnO[i, :, 0:1024] = np.broadcast_to(f(inp["od_ln_g"])[i][None, :], (128, 1024))
        lnO[i, :, 1024:2048] = np.broadcast_to(f(inp["od_ln_b"])[i][None, :], (128, 1024))
    perm = np.zeros((128, 256), np.float32)
    for p in range(128):
        d = p % 64
        if pa[d] >= 0:
            perm[(p // 64) * 64 + pa[d], p] = 1.0
        perm[(p // 64) * 64 + pb[d], 128 + p] = 1.0
    cst = np.zeros((128, 256), np.float32)
    cst[:, 0:128] = np.eye(128, dtype=np.float32)
    cst[0:64, 128:192] = 1.0
    cst[64:128, 192:256] = 1.0
    return dict(wA=wA, wBk=wBk, wBq=wBq, wEo=wEo, wG1=wG1, wG2=wG2, wQb=wQb, wKvb=wKvb, wOo=wOo,
                colsE=colsE, lamE=lamE, lnE=lnE, colsO=colsO, lnO=lnO, rope=_rope_tables(), cst=cst, perm=perm)


_CACHE = {}


def run_layers(x_full, weights, layers, cores=None, trace=False):
    key = tuple(layers)
    if key not in _CACHE:
        _CACHE[key] = build_program(list(layers))
    nc, _ = _CACHE[key]
    n = x_full.shape[0]
    in_maps = []
    for b in range(n):
        m = dict(weights)
        m["x"] = np.ascontiguousarray(x_full[b], dtype=np.float32)
        in_maps.append(m)
    res = run_bass_kernel_spmd(nc, in_maps, core_ids=list(range(n)), **({"trace": True} if trace else {}))
    return np.stack([r["out"] for r in res.results], axis=0), res


def kernel(**inputs):
    x = np.asarray(inputs["x"], dtype=np.float32)
    weights = _prep_weights(inputs)
    out, _ = run_layers(x, weights, (0, 1, 2, 3))
    return out.astype(np.float32)
```
